# Optimizing a Trainium2 kernel written in Bass

```python
import math
import jax, jax.numpy as jnp
from jax import lax
import numpy as np

D_MODEL = 2048
BATCH = 2
SEQ = 8192
DEPTH = 2

HEAD_DIM = 128
RET_HEADS = 6
RET_W = RET_HEADS * HEAD_DIM
RET_CHUNK = 128
DIFF_HEADS = 4
DIFF_QK = HEAD_DIM // 2
DIFF_W = DIFF_HEADS * HEAD_DIM
Q_BLOCK = 128
LRU_W = 768
LRU_BLOCKS = 8
LRU_BW = LRU_W // LRU_BLOCKS
CONV_W = 4
LRU_C = 8.0
D_MIX = RET_W + DIFF_W + LRU_W
SPLIT_SIZES = (RET_W, RET_W, RET_W, RET_W,
               DIFF_W, DIFF_W, DIFF_W, DIFF_W,
               LRU_W, LRU_W)
D_IN = sum(SPLIT_SIZES)
ROPE_THETA = 10000.0
EPS = 1e-6

kernel_name = "hybrid_retention_diffattn_rglru_block"


def rms_norm(x, g=None):
    x32 = x.astype(jnp.float32)
    y = x32 * lax.rsqrt(jnp.mean(x32 * x32, axis=-1, keepdims=True) + EPS)
    if g is not None:
        y = y * g.astype(jnp.float32)
    return y.astype(x.dtype)


def rope(x, positions, inv_freq):
    ang = positions.astype(jnp.float32)[..., None] * inv_freq
    cos = jnp.cos(ang)[:, :, None, :].astype(x.dtype)
    sin = jnp.sin(ang)[:, :, None, :].astype(x.dtype)
    x1, x2 = jnp.split(x, 2, axis=-1)
    return jnp.concatenate([x1 * cos - x2 * sin, x2 * cos + x1 * sin], axis=-1)


def retention(q, k, v, positions):
    B, S, H, D = q.shape
    dt = q.dtype
    inv = 1.0 / (ROPE_THETA ** jnp.linspace(0.0, 1.0, D // 2, dtype=jnp.float32))
    q = rope(q, positions, inv)
    k = rope(k, positions, inv) * (D ** -0.5)
    log_g = jnp.log1p(-jnp.power(2.0, -5.0 - jnp.arange(H, dtype=jnp.float32)))
    C = RET_CHUNK
    N = S // C
    idx = jnp.arange(C, dtype=jnp.float32)
    rel = idx[:, None] - idx[None, :]
    intra_decay = jnp.where(rel[None] >= 0,
                            jnp.exp(jnp.maximum(rel, 0.0)[None] * log_g[:, None, None]),
                            0.0).astype(dt)
    qc = q.reshape(B, N, C, H, D)
    kc = k.reshape(B, N, C, H, D)
    vc = v.reshape(B, N, C, H, D)
    scores = jnp.einsum('bnihd,bnjhd->bnhij', qc, kc) * intra_decay
    o_intra = jnp.einsum('bnhij,bnjhe->bnihe', scores, vc)
    k_w = jnp.exp((C - 1.0 - idx)[:, None] * log_g[None, :]).astype(dt)
    kv = jnp.einsum('bnjhd,bnjhe->nbhde', kc * k_w[:, :, None], vc)
    chunk_decay = jnp.exp(C * log_g)[None, :, None, None].astype(kv.dtype)

    def step(state, kv_n):
        return state * chunk_decay + kv_n, state

    _, s_prev = lax.scan(step, jnp.zeros_like(kv[0]), kv)
    q_w = jnp.exp((idx + 1.0)[:, None] * log_g[None, :]).astype(dt)
    o_cross = jnp.einsum('bnihd,nbhde->bnihe', qc * q_w[:, :, None], s_prev)
    return (o_intra + o_cross).reshape(B, S, H, D)


def diff_attention(q, k, v, positions, lam):
    B, S, H, _, DQ = q.shape
    inv = 1.0 / (ROPE_THETA ** (jnp.arange(0, DQ, 2, dtype=jnp.float32) / DQ))
    q = rope(q.reshape(B, S, H * 2, DQ), positions, inv).reshape(B, S, H, 2, DQ) * (DQ ** -0.5)
    k = rope(k.reshape(B, S, H * 2, DQ), positions, inv).reshape(B, S, H, 2, DQ)
    kpos = jnp.arange(S)

    def block(i):
        qb = lax.dynamic_slice_in_dim(q, i * Q_BLOCK, Q_BLOCK, axis=1)
        s = jnp.einsum('bqhcd,bkhcd->bhcqk', qb, k).astype(jnp.float32)
        qpos = i * Q_BLOCK + jnp.arange(Q_BLOCK)
        s = jnp.where(kpos[None, :] <= qpos[:, None], s, -jnp.inf)
        p = jax.nn.softmax(s, axis=-1)
        w = p[:, :, 0] - lam * p[:, :, 1]
        return jnp.einsum('bhqk,bkhe->bqhe', w.astype(v.dtype), v)

    o = lax.map(block, jnp.arange(S // Q_BLOCK))
    return jnp.transpose(o, (1, 0, 2, 3, 4)).reshape(B, S, H, v.shape[-1])


def rg_lru(x, conv_w, conv_b, wa, ba, wx, bx, lam):
    B, S, W = x.shape
    xp = jnp.pad(x, ((0, 0), (CONV_W - 1, 0), (0, 0)))
    xc = conv_b + xp[:, 0:S] * conv_w[0]
    for j in range(1, CONV_W):
        xc = xc + xp[:, j:j + S] * conv_w[j]
    xb = xc.reshape(B, S, LRU_BLOCKS, LRU_BW)
    r = jax.nn.sigmoid(jnp.einsum('bsnk,nkj->bsnj', xb, wa).reshape(B, S, W) + ba)
    i = jax.nn.sigmoid(jnp.einsum('bsnk,nkj->bsnj', xb, wx).reshape(B, S, W) + bx)
    log_a = (LRU_C * r.astype(jnp.float32)) * jax.nn.log_sigmoid(lam.astype(jnp.float32))
    a = jnp.exp(log_a)
    mult = jnp.sqrt(-jnp.expm1(2.0 * log_a))
    b = mult * (i * xc).astype(jnp.float32)

    def combine(left, right):
        a1, b1 = left
        a2, b2 = right
        return a1 * a2, a2 * b1 + b2

    _, h = lax.associative_scan(combine, (a, b), axis=1)
    return h.astype(x.dtype)


def setup_inputs(seed: int = 0) -> dict:
    key = jax.random.key(seed)
    ks = jax.random.split(key, 20)
    f32 = jnp.float32
    x = jax.random.normal(ks[0], (BATCH, SEQ, D_MODEL), f32)
    positions = jnp.broadcast_to(jnp.arange(SEQ, dtype=jnp.int32), (BATCH, SEQ))
    pre_norm_g = 1.0 + 0.05 * jax.random.normal(ks[1], (DEPTH, D_MODEL), f32)
    w_in = jax.random.normal(ks[2], (DEPTH, D_MODEL, D_IN), f32) * (D_MODEL ** -0.5)
    diff_lambda_q1 = 0.1 * jax.random.normal(ks[3], (DEPTH, DIFF_QK), f32)
    diff_lambda_k1 = 0.1 * jax.random.normal(ks[4], (DEPTH, DIFF_QK), f32)
    diff_lambda_q2 = 0.1 * jax.random.normal(ks[5], (DEPTH, DIFF_QK), f32)
    diff_lambda_k2 = 0.1 * jax.random.normal(ks[6], (DEPTH, DIFF_QK), f32)
    diff_subln_g = 1.0 + 0.05 * jax.random.normal(ks[7], (DEPTH, HEAD_DIM), f32)
    lru_conv_w = jax.random.normal(ks[8], (DEPTH, CONV_W, LRU_W), f32) * (CONV_W ** -0.5)
    lru_conv_b = 0.01 * jax.random.normal(ks[9], (DEPTH, LRU_W), f32)
    lru_wa = jax.random.normal(ks[10], (DEPTH, LRU_BLOCKS, LRU_BW, LRU_BW), f32) * (LRU_BW ** -0.5)
    lru_ba = 0.01 * jax.random.normal(ks[11], (DEPTH, LRU_W), f32)
    lru_wx = jax.random.normal(ks[12], (DEPTH, LRU_BLOCKS, LRU_BW, LRU_BW), f32) * (LRU_BW ** -0.5)
    lru_bx = 0.01 * jax.random.normal(ks[13], (DEPTH, LRU_W), f32)
    u = jax.random.uniform(ks[14], (DEPTH, LRU_W), f32, minval=0.9, maxval=0.999)
    s = u ** (1.0 / LRU_C)
    lru_lambda = jnp.log(s) - jnp.log1p(-s)
    w_out = jax.random.normal(ks[15], (DEPTH, D_MIX, D_MODEL), f32) * (D_MIX ** -0.5)
    post_norm_g = 1.0 + 0.05 * jax.random.normal(ks[16], (DEPTH, D_MODEL), f32)
    return {"x": x, "positions": positions, "pre_norm_g": pre_norm_g, "w_in": w_in,
            "diff_lambda_q1": diff_lambda_q1, "diff_lambda_k1": diff_lambda_k1,
            "diff_lambda_q2": diff_lambda_q2, "diff_lambda_k2": diff_lambda_k2,
            "diff_subln_g": diff_subln_g, "lru_conv_w": lru_conv_w, "lru_conv_b": lru_conv_b,
            "lru_wa": lru_wa, "lru_ba": lru_ba, "lru_wx": lru_wx, "lru_bx": lru_bx,
            "lru_lambda": lru_lambda, "w_out": w_out, "post_norm_g": post_norm_g}


def reference(x, positions, pre_norm_g, w_in, diff_lambda_q1, diff_lambda_k1,
              diff_lambda_q2, diff_lambda_k2, diff_subln_g, lru_conv_w, lru_conv_b,
              lru_wa, lru_ba, lru_wx, lru_bx, lru_lambda, w_out, post_norm_g):
    B, S, _ = x.shape
    split_idx = [int(v) for v in np.cumsum(SPLIT_SIZES)[:-1]]
    for l in range(DEPTH):
        h = rms_norm(x, pre_norm_g[l])
        proj = jnp.einsum('bsd,de->bse', h, w_in[l])
        (rq, rk, rv, rg, dq, dk, dv, dg, lx, lg) = jnp.split(proj, split_idx, axis=-1)

        ro = retention(rq.reshape(B, S, RET_HEADS, HEAD_DIM),
                       rk.reshape(B, S, RET_HEADS, HEAD_DIM),
                       rv.reshape(B, S, RET_HEADS, HEAD_DIM), positions)
        ret_out = rms_norm(ro).reshape(B, S, RET_W) * jax.nn.silu(rg)

        lam_init = 0.8 - 0.6 * math.exp(-0.3 * l)
        lam = (jnp.exp(jnp.sum(diff_lambda_q1[l] * diff_lambda_k1[l]).astype(jnp.float32))
               - jnp.exp(jnp.sum(diff_lambda_q2[l] * diff_lambda_k2[l]).astype(jnp.float32))
               + lam_init)
        do = diff_attention(dq.reshape(B, S, DIFF_HEADS, 2, DIFF_QK),
                            dk.reshape(B, S, DIFF_HEADS, 2, DIFF_QK),
                            dv.reshape(B, S, DIFF_HEADS, HEAD_DIM), positions, lam)
        diff_out = (rms_norm(do, diff_subln_g[l]) * (1.0 - lam_init)).reshape(B, S, DIFF_W) * jax.nn.silu(dg)

        lru_out = rg_lru(lx, lru_conv_w[l], lru_conv_b[l], lru_wa[l], lru_ba[l],
                         lru_wx[l], lru_bx[l], lru_lambda[l]) * jax.nn.silu(lg)

        mixed = jnp.concatenate([ret_out, diff_out, lru_out], axis=-1)
        y = jnp.einsum('bse,ed->bsd', mixed, w_out[l])
        x = x + rms_norm(y, post_norm_g[l])
    return x
```

```python
import math
from contextlib import ExitStack

import numpy as np
import ml_dtypes

import concourse.bass as bass
import concourse.mybir as mybir
from concourse.bass_utils import run_bass_kernel_spmd

F32 = mybir.dt.float32
BF16 = mybir.dt.bfloat16
I32 = mybir.dt.int32
AF = mybir.ActivationFunctionType
ALU = mybir.AluOpType
AX = mybir.AxisListType

NCORES = 8
NR = 4
S = 8192
D = 2048
T = 2048
NTT = 16
L = 2
DIN = 6656
EPS = 1e-6
XD_ROWS = 2304
XO_ROWS = 1280
GROUPS = [[0, 1, 2, 3], [4, 5, 6, 7]]

COLG = [
    ("rk", 768, 512, 0), ("rk", 1280, 256, 4),
    ("rv", 1536, 512, 0), ("rv", 2048, 256, 4),
    ("rq", 0, 512, 0), ("rq", 512, 256, 4),
    ("rg", 2304, 512, 0), ("rg", 2816, 256, 4),
    ("dq", 3072, 512, 0), ("dk", 3584, 512, 0), ("dv", 4096, 512, 0),
    ("lx", 5120, 384, 0), ("lx", 5504, 384, 4),
    ("dg", 4608, 512, 0),
    ("lg", 5888, 384, 0), ("lg", 6272, 384, 4),
]


class KB:
    def __init__(self, nc):
        self.nc = nc
        self.eng = dict(pe=nc.tensor, act=nc.scalar, dve=nc.vector, pool=nc.gpsimd, sp=nc.sync)
        self.sem = {e: nc.alloc_semaphore("sem_" + e) for e in ("pe", "act", "dve", "pool")}
        self.cnt = {e: 0 for e in self.sem}
        self.last = {e: (None, None) for e in self.sem}
        self.waited = {e: {} for e in self.eng}

    def op(self, e, name, *args, w=(), m=False, **kw):
        for t in w:
            self.wait(e, t)
        ins = getattr(self.eng[e], name)(*args, **kw)
        tok = None
        if m:
            self.cnt[e] += 1
            ins.then_inc(self.sem[e], 1)
            tok = ("c", e, self.cnt[e])
        self.last[e] = (ins, tok)
        return tok

    def mark_last(self, e):
        ins, tok = self.last[e]
        if ins is None:
            return None
        if tok is None:
            self.cnt[e] += 1
            ins.then_inc(self.sem[e], 1)
            tok = ("c", e, self.cnt[e])
            self.last[e] = (ins, tok)
        return tok

    def wait(self, e, t):
        if t is None:
            return
        if isinstance(t, list):
            for tt in t:
                self.wait(e, tt)
            return
        if t[0] == "c":
            key, val, sem = t[1], t[2], self.sem[t[1]]
        else:
            key, val, sem = t[3], t[2], t[1]
        if self.waited[e].get(key, 0) >= val:
            return
        self.waited[e][key] = val
        self.eng[e].wait_ge(sem, val)

    def barrier(self, rings=()):
        toks = [self.mark_last(e) for e in ("pe", "act", "dve", "pool")]
        for r in rings:
            toks += r.all_toks()
        for e in self.eng:
            for t in toks:
                self.wait(e, t)


class Ring:
    def __init__(self, kb, name, n):
        self.kb = kb
        self.name = name
        self.n = n
        self.sems = [kb.nc.alloc_semaphore(f"{name}{i}") for i in range(n)]
        self.cnt = [0] * n

    def dma(self, e, slot, out, in_, w=(), **kw):
        for t in w:
            self.kb.wait(e, t)
        ins = self.kb.eng[e].dma_start(out=out, in_=in_, **kw)
        self.cnt[slot] += 16
        ins.then_inc(self.sems[slot], 16)
        return self.tok(slot)

    def tok(self, slot):
        if self.cnt[slot] == 0:
            return None
        return ("d", self.sems[slot], self.cnt[slot], (self.name, slot))

    def all_toks(self):
        return [self.tok(s) for s in range(self.n)]


def build(nlayers=L, stop_after=None, debug=False):
    nc = bass.Bass("TRN2", target_bir_lowering=False)
    kb = KB(nc)
    dbg_kind = "ExternalOutput" if debug else "Internal"

    ucnt = [0]

    def uniq(name):
        ucnt[0] += 1
        return f"{name}_u{ucnt[0]}"

    def din(name, shape, dt):
        return nc.dram_tensor(name, shape, dt, kind="ExternalInput")

    x_in = din("x_c", [T, D], F32)
    pos_in = din("pos_c", [128, NTT], I32)
    w_in_s = din("w_in_s", [L, D // NCORES, DIN], F32)
    w_out_s = din("w_out_s", [L, D // NCORES, D], F32)
    pre_g = din("pre_g", [L, D], F32)
    post_g = din("post_g", [L, D], F32)
    dlam = din("dlam", [L, 4, 64], F32)
    subg = din("subg", [L, 128], F32)
    lru_w = din("lru_w", [L, 2, 2, 96, 96], F32)
    lru_p = din("lru_p", [L, 2, 96, 8], F32)
    c_ident = din("c_ident", [128, 128], BF16)
    c_mask = din("c_mask", [128, 128], BF16)
    c_invf = din("c_invf", [96], F32)
    c_dec = din("c_dec", [128, 12], F32)
    c_gc = din("c_gc", [6], F32)
    c_cf = din("c_cf", [18], F32)
    out_t = nc.dram_tensor("out", [T, D], F32, kind="ExternalOutput")

    x1 = nc.dram_tensor("x1", [T, D], F32, kind=dbg_kind)
    rqT = nc.dram_tensor("rqT", [6, 128, T], BF16, kind=dbg_kind)
    rkT = nc.dram_tensor("rkT", [6, 128, T], BF16, kind=dbg_kind)
    rk = nc.dram_tensor("rk", [T, 768], BF16, kind=dbg_kind)
    rv = nc.dram_tensor("rv", [T, 768], BF16, kind=dbg_kind)
    rg = nc.dram_tensor("rg", [T, 768], BF16, kind=dbg_kind)
    dg = nc.dram_tensor("dg", [T, 512], BF16, kind=dbg_kind)
    lgT = nc.dram_tensor("lgT", [8, 96, T], BF16, kind=dbg_kind)
    mixr = nc.dram_tensor("mixr", [6, 128, T], BF16, kind=dbg_kind)
    XD = nc.dram_tensor("XD", [XD_ROWS, 2048], BF16)
    XO = nc.dram_tensor("XO", [XO_ROWS, 2048], BF16)
    RS = nc.dram_tensor("RS", [768, 128], F32)
    XDG = [nc.dram_tensor(f"XDG{l}", [NCORES * XD_ROWS, 2048], BF16) for l in range(L)]
    XOG = [nc.dram_tensor(f"XOG{l}", [NCORES * XO_ROWS, 2048], BF16) for l in range(L)]
    RSG = [nc.dram_tensor(f"RSG{l}", [NR * 768, 128], F32) for l in range(L)]
    wib = [nc.dram_tensor(f"wib{l}", [D // NCORES, DIN], BF16) for l in range(L)]
    wob = [nc.dram_tensor(f"wob{l}", [D // NCORES, D], BF16) for l in range(L)]
    WI = [nc.dram_tensor(f"WI{l}", [D, DIN], BF16) for l in range(L)]
    WO = [nc.dram_tensor(f"WO{l}", [D, D], BF16) for l in range(L)]
    XDL = nc.dram_tensor("XDL", [NR, 576, 2048], BF16, kind=dbg_kind)
    XOL = nc.dram_tensor("XOL", [NR, 320, 2048], BF16, kind=dbg_kind)
    dbgs = {}
    if debug:
        dbgs["XDd"] = nc.dram_tensor("XDd", [XD_ROWS, 2048], BF16, kind="ExternalOutput")
        dbgs["XOd"] = nc.dram_tensor("XOd", [XO_ROWS, 2048], BF16, kind="ExternalOutput")

    cc_sem = nc.alloc_semaphore("cc_sem")
    cc_cnt = [0]
    fin = Ring(kb, "fin", 1)

    with ExitStack() as gs:
        def gsb(name, shape, dt):
            return gs.enter_context(nc.sbuf_tensor(name, shape, dt))

        ident = gsb("ident", [128, 128], BF16)
        maskT = gsb("maskT", [128, 128], BF16)
        COS = gsb("COS", [128, NTT, 96], F32)
        SIN = gsb("SIN", [128, NTT, 96], F32)
        dec = gsb("dec", [128, 12], F32)
        gcb = gsb("gcb", [128, 6], F32)
        cfb = gsb("cfb", [128, 18], F32)
        ld = Ring(kb, "ld", 1)

        with ExitStack() as es:
            def sb(name, shape, dt):
                return es.enter_context(nc.sbuf_tensor(uniq(name), shape, dt))
            posi = sb("posi", [128, NTT], I32)
            posf = sb("posf", [128, NTT], F32)
            invb = sb("invb", [128, 96], F32)
            ANG = sb("ANG", [128, NTT * 96], F32)
            A2 = sb("A2", [128, NTT * 96], F32)
            KF = sb("KF", [128, NTT * 96], F32)
            KI = sb("KI", [128, NTT * 96], I32)
            ld.dma("sp", 0, ident[:, :], c_ident[:, :])
            ld.dma("sp", 0, maskT[:, :], c_mask[:, :])
            ld.dma("sp", 0, dec[:, :], c_dec[:, :])
            ld.dma("sp", 0, gcb[:, :], c_gc.ap().partition_broadcast(128))
            ld.dma("sp", 0, cfb[:, :], c_cf.ap().partition_broadcast(128))
            ld.dma("sp", 0, posi[:, :], pos_in[:, :])
            t_ld = ld.dma("sp", 0, invb[:, :], c_invf.ap().partition_broadcast(128))
            t = kb.op("dve", "tensor_copy", out=posf[:, :], in_=posi[:, :], w=[t_ld], m=True)
            for tt in range(NTT):
                t2 = kb.op("dve", "tensor_scalar", out=ANG[:, tt * 96:(tt + 1) * 96], in0=invb[:, :],
                           scalar1=posf[:, tt:tt + 1], scalar2=None, op0=ALU.mult, w=[t], m=(tt == NTT - 1))
            TWO_PI = 2.0 * math.pi
            C1 = 6.28125
            C2 = TWO_PI - C1

            def sincos(dst, shift, t_in):
                ta = kb.op("dve", "tensor_scalar", out=KF[:, :], in0=ANG[:, :], scalar1=float(1.0 / TWO_PI),
                           scalar2=float(shift / TWO_PI), op0=ALU.mult, op1=ALU.add, w=[t_in], m=True)
                tb = kb.op("dve", "tensor_copy", out=KI[:, :], in_=KF[:, :], w=[ta], m=True)
                tc = kb.op("dve", "tensor_copy", out=KF[:, :], in_=KI[:, :], w=[tb], m=True)
                td = kb.op("dve", "scalar_tensor_tensor", out=A2[:, :], in0=KF[:, :], scalar=float(-C1), in1=ANG[:, :],
                           op0=ALU.mult, op1=ALU.add, w=[tc], m=True)
                te = kb.op("dve", "scalar_tensor_tensor", out=A2[:, :], in0=KF[:, :], scalar=float(-C2), in1=A2[:, :],
                           op0=ALU.mult, op1=ALU.add, w=[td], m=True)
                tf = kb.op("dve", "tensor_scalar", out=A2[:, :], in0=A2[:, :], scalar1=float(shift), scalar2=None,
                           op0=ALU.add, w=[te], m=True)
                tg = kb.op("dve", "tensor_scalar", out=A2[:, :], in0=A2[:, :], scalar1=3.1415925, scalar2=-3.1415925,
                           op0=ALU.min, op1=ALU.max, w=[tf], m=True)
                th = kb.op("act", "activation", out=dst[:, :, :].rearrange("p a b -> p (a b)"), in_=A2[:, :],
                           func=AF.Sin, w=[tg], m=True)
                return th

            t_s = sincos(SIN, 0.0, t2)
            t_c = sincos(COS, math.pi / 2.0, t_s)
            kb.barrier([ld])

        rcast = Ring(kb, "rcast", 1)
        t_wi, t_wo = [], []
        for l in range(L):
            for cc_ in range(4):
                tcs = rcast.dma("pool", 0, wib[l][:, cc_ * 1664:(cc_ + 1) * 1664],
                                w_in_s[l, :, cc_ * 1664:(cc_ + 1) * 1664])
            tcs = rcast.dma("pool", 0, wob[l][:, :], w_out_s[l, :, :])
        ALL8 = [list(range(NCORES))]

        def allgather(src, dst, wait_toks, groups=GROUPS):
            for t_ in wait_toks:
                kb.wait("pool", t_)
            ins = nc.gpsimd.collective_compute(
                "AllGather", ALU.bypass, replica_groups=groups,
                ins=[src.ap().opt()], outs=[dst.ap().opt()])
            cc_cnt[0] += 1
            ins.then_inc(cc_sem, 1)
            return ("d", cc_sem, cc_cnt[0], ("cc",))

        for l in range(L):
            t_wi.append(allgather(wib[l], WI[l], [tcs], groups=ALL8))
            t_wo.append(allgather(wob[l], WO[l], [tcs], groups=ALL8))
        rcp = Ring(kb, "rcp", 1)
        pid_p = nc.gpsimd.partition_id()

        def copy_local(src_g, dst_l, rows, t_ag):
            base = (pid_p // 4) * (4 * 4 * rows) + (pid_p % 4) * rows
            flat = src_g.ap()
            src = bass.AP(flat.tensor, base * 2048, [[4 * rows * 2048, NR], [2048, rows], [1, 2048]])
            return rcp.dma("pool", 0, dst_l.ap(), src, w=[t_ag])

        def phase1(l, x_src):
            with ExitStack() as es:
                def sb(name, shape, dt):
                    return es.enter_context(nc.sbuf_tensor(uniq(name), shape, dt))

                def psb(name, shape, dt):
                    return es.enter_context(nc.psum_tensor(uniq(name), shape, dt))
                hT = sb("hT", [128, 16, T], BF16)
                wt = [sb(f"wt{i}", [128, 16, 512], BF16) for i in range(2)]
                xt = [sb(f"xt{i}", [128, D], F32) for i in range(2)]
                hb = sb("hb", [128, D], BF16)
                gbc = sb("gbc", [128, D], F32)
                junk = sb("junk", [128, D], BF16)
                ssq = sb("ssq", [128, NTT], F32)
                ms = sb("ms", [128, NTT], F32)
                lnv = sb("lnv", [128, NTT], F32)
                rstd = sb("rstd", [128, NTT], F32)
                X32 = [sb(f"X32_{i}", [128, 768], F32) for i in range(2)]
                tA = sb("tA", [128, 384], F32)
                tB = sb("tB", [128, 384], F32)
                tC = sb("tC", [128, 384], F32)
                tD = sb("tD", [128, 384], F32)
                O16 = [sb(f"O16_{i}", [128, 512], BF16) for i in range(3)]
                OT16 = [sb(f"OT16_{i}", [128, 512], BF16) for i in range(2)]
                F16 = [sb(f"F16_{i}", [96, 512], BF16) for i in range(2)]
                pproj = [psb(f"pproj{i}", [128, 512], F32) for i in range(3)]
                ptq = [psb(f"ptq{i}", [128, 1024], BF16) for i in range(2)]
                ptr = [psb(f"ptr{i}", [128, 1024], BF16) for i in range(2)]
                rx = Ring(kb, f"rx{l}_", 2)
                rw = Ring(kb, f"rw{l}_", 2)
                rO = Ring(kb, f"rO{l}_", 3)
                rOT = Ring(kb, f"rOT{l}_", 2)
                rF = Ring(kb, f"rF{l}_", 2)
                rg_ = Ring(kb, f"rgl{l}_", 1)

                t_g = rg_.dma("sp", 0, gbc[:, :], pre_g[l, :].partition_broadcast(128))
                ngr = len(COLG)

                def load_w(gi, waits):
                    kind, c0, w, h0 = COLG[gi]
                    return rw.dma("pool", gi % 2, wt[gi % 2][:, :, 0:w],
                                  WI[l][:, c0:c0 + w].rearrange("(k p) c -> p k c", p=128), w=waits + [t_wi[l]])
                wtok = {0: load_w(0, [])}
                wt_last_mm = {}

                st = dict(u=0, xs=0, o=0, ot=0, f=0, tq=0)
                bank_free = [None, None, None]
                x32_free = [None, None]
                o_free = [[], [], []]
                ptq_free = [None, None]
                hT_tok = {}
                xt_free = [[], []]
                hb_free = [None]
                ptr_free = [None]

                def norm_tile(tt):
                    s = tt % 2
                    t_x = rx.dma("sp", s, xt[s][:, :], x_src[tt * 128:(tt + 1) * 128, :], w=xt_free[s])
                    t_sq = kb.op("act", "activation", out=junk[:, :], in_=xt[s][:, :], func=AF.Square,
                                 accum_out=ssq[:, tt:tt + 1], w=[t_x], m=True)
                    t_ms = kb.op("dve", "tensor_scalar", out=ms[:, tt:tt + 1], in0=ssq[:, tt:tt + 1], scalar1=1.0 / D,
                                 scalar2=EPS, op0=ALU.mult, op1=ALU.add, w=[t_sq], m=True)
                    t_ln = kb.op("act", "activation", out=lnv[:, tt:tt + 1], in_=ms[:, tt:tt + 1], func=AF.Ln,
                                 w=[t_ms], m=True)
                    t_rs = kb.op("act", "activation", out=rstd[:, tt:tt + 1], in_=lnv[:, tt:tt + 1], func=AF.Exp,
                                 scale=-0.5, w=[t_ln], m=True)
                    t_h = kb.op("dve", "scalar_tensor_tensor", out=hb[:, :], in0=xt[s][:, :],
                                scalar=rstd[:, tt:tt + 1], in1=gbc[:, :], op0=ALU.mult, op1=ALU.mult,
                                w=[t_rs, t_g, hb_free[0]], m=True)
                    xt_free[s] = [t_sq, t_h]
                    for k in range(16):
                        t_tr = kb.op("pe", "transpose", out=ptr[k // 8][:, (k % 8) * 128:(k % 8 + 1) * 128],
                                     in_=hb[:, k * 128:(k + 1) * 128], identity=ident[:, :],
                                     w=[t_h, ptr_free[0]], m=(k == 15))
                    hb_free[0] = t_tr
                    for hh in range(2):
                        t_ev = kb.op("act", "tensor_copy" if False else "activation",
                                     out=hT[:, hh * 8:(hh + 1) * 8, tt * 128:(tt + 1) * 128],
                                     in_=ptr[hh][:, :].rearrange("p (a b) -> p a b", a=8), func=AF.Copy,
                                     w=[t_tr], m=True)
                    ptr_free[0] = t_ev
                    hT_tok[tt] = t_ev

                def rope(xs, oslot, w, tt, nfreq, tab0, t_in):
                    nb = w // (2 * nfreq)
                    hw = w // 2
                    Xv = X32[xs][:, 0:w].rearrange("p (h two f) -> p h two f", two=2, f=nfreq)
                    Ov = O16[oslot][:, 0:w].rearrange("p (h two f) -> p h two f", two=2, f=nfreq)
                    X0, X1 = Xv[:, :, 0, :], Xv[:, :, 1, :]
                    Cb = COS[:, tt, tab0:tab0 + nfreq].unsqueeze(1).to_broadcast([128, nb, nfreq])
                    Sb = SIN[:, tt, tab0:tab0 + nfreq].unsqueeze(1).to_broadcast([128, nb, nfreq])

                    def tv(tl):
                        return tl[:, 0:hw].rearrange("p (h f) -> p h f", f=nfreq)
                    kb.op("dve", "tensor_tensor", out=tv(tA), in0=X1, in1=Sb, op=ALU.mult, w=[t_in] + o_free[oslot])
                    t1 = kb.op("dve", "tensor_tensor", out=tv(tB), in0=X0, in1=Cb, op=ALU.mult, m=True)
                    kb.op("dve", "tensor_tensor", out=tv(tC), in0=X0, in1=Sb, op=ALU.mult)
                    t2_ = kb.op("dve", "tensor_tensor", out=tv(tD), in0=X1, in1=Cb, op=ALU.mult, m=True)
                    kb.op("dve", "tensor_tensor", out=Ov[:, :, 0, :], in0=tv(tB), in1=tv(tA), op=ALU.subtract, w=[t1])
                    t3 = kb.op("dve", "tensor_tensor", out=Ov[:, :, 1, :], in0=tv(tD), in1=tv(tC), op=ALU.add,
                               w=[t2_], m=True)
                    x32_free[xs] = t2_
                    return t3

                def transposes_out(oslot, w, t_in, dst_ap_fn, tt):
                    nh = w // 128
                    q = st["tq"] % 2
                    st["tq"] += 1
                    for hh in range(nh):
                        t_tr = kb.op("pe", "transpose", out=ptq[q][:, hh * 128:(hh + 1) * 128],
                                     in_=O16[oslot][:, hh * 128:(hh + 1) * 128], identity=ident[:, :],
                                     w=[t_in, ptq_free[q]], m=(hh == nh - 1))
                    os_ = st["ot"] % 2
                    st["ot"] += 1
                    t_ot = kb.op("dve", "tensor_copy", out=OT16[os_][:, 0:w], in_=ptq[q][:, 0:w],
                                 w=[t_tr, rOT.tok(os_)], m=True)
                    ptq_free[q] = t_ot
                    rOT.dma("sp", os_, dst_ap_fn(nh, tt), OT16[os_][:, 0:w].rearrange("p (h t) -> p h t", t=128),
                            w=[t_ot])
                    return t_tr

                XD2 = XD.ap()

                def unit_tokmajor(gi, tt):
                    kind, c0, w, h0 = COLG[gi]
                    u = st["u"]
                    st["u"] += 1
                    pb = pproj[u % 3]
                    waits = [wtok[gi], hT_tok[tt], bank_free[u % 3]]
                    for k in range(16):
                        t_mm = kb.op("pe", "matmul", pb[:, 0:w], lhsT=hT[:, k, tt * 128:(tt + 1) * 128],
                                     rhs=wt[gi % 2][:, k, 0:w], start=(k == 0), stop=(k == 15),
                                     w=(waits if k == 0 else []), m=(k == 15))
                    wt_last_mm[gi] = t_mm
                    nh = w // 128
                    osl = st["o"] % 3
                    st["o"] += 1
                    if kind in ("rq", "rk"):
                        xs = st["xs"] % 2
                        st["xs"] += 1
                        dcol = (0 if kind == "rq" else 6) + h0
                        for hh in range(nh):
                            t_e = kb.op("act", "activation", out=X32[xs][:, hh * 128:(hh + 1) * 128],
                                        in_=pb[:, hh * 128:(hh + 1) * 128], func=AF.Copy,
                                        scale=dec[:, dcol + hh:dcol + hh + 1], w=[t_mm, x32_free[xs]],
                                        m=(hh == nh - 1))
                        bank_free[u % 3] = t_e
                        t_r = rope(xs, osl, w, tt, 64, 0, t_e)
                        o_free[osl] = []
                        if kind == "rk":
                            o_free[osl].append(rO.dma("sp", osl, rk[tt * 128:(tt + 1) * 128, h0 * 128:h0 * 128 + w],
                                                      O16[osl][:, 0:w], w=[t_r]))
                        dstT = rqT if kind == "rq" else rkT
                        t_tr = transposes_out(
                            osl, w, t_r,
                            lambda nh_, tt_: dstT[h0:h0 + nh_, :, tt_ * 128:(tt_ + 1) * 128].rearrange("h d t -> d h t"),
                            tt)
                        o_free[osl].append(t_tr)
                    elif kind in ("dq", "dk"):
                        xs = st["xs"] % 2
                        st["xs"] += 1
                        t_e = kb.op("act", "activation", out=X32[xs][:, 0:w], in_=pb[:, 0:w], func=AF.Copy,
                                    w=[t_mm, x32_free[xs]], m=True)
                        bank_free[u % 3] = t_e
                        t_r = rope(xs, osl, w, tt, 32, 64, t_e)
                        r0 = 0 if kind == "dq" else 128
                        t_tr = transposes_out(
                            osl, w, t_r,
                            lambda nh_, tt_: XD2.rearrange("(h r) c -> r h c", h=4)[
                                r0:r0 + 128, :, tt_ * 128:(tt_ + 1) * 128],
                            tt)
                        o_free[osl] = [t_tr]
                    else:
                        func = AF.Silu if kind in ("rg", "dg") else AF.Copy
                        t_e = kb.op("act", "activation", out=O16[osl][:, 0:w], in_=pb[:, 0:w], func=func,
                                    w=[t_mm] + o_free[osl], m=True)
                        bank_free[u % 3] = t_e
                        if kind == "dv":
                            dst = XD2.rearrange("(h r) c -> h r c", h=4)[
                                :, 256 + tt * 8:256 + tt * 8 + 8, :].rearrange("h r (a e) -> (r a) h e", e=128)
                            src = O16[osl][:, 0:w].rearrange("p (h e) -> p h e", e=128)
                        else:
                            dd = {"rv": rv, "rg": rg, "dg": dg}[kind]
                            dst = dd[tt * 128:(tt + 1) * 128, h0 * 128:h0 * 128 + w]
                            src = O16[osl][:, 0:w]
                        o_free[osl] = [rO.dma("sp", osl, dst, src, w=[t_e])]

                def unit_featmajor(gi, n, tg):
                    kind, c0, w, h0 = COLG[gi]
                    u = st["u"]
                    st["u"] += 1
                    pb = pproj[u % 3]
                    waits = [wtok[gi], bank_free[u % 3]]
                    for k in range(16):
                        t_mm = kb.op("pe", "matmul", pb[0:96, 0:512], lhsT=wt[gi % 2][:, k, n * 96:(n + 1) * 96],
                                     rhs=hT[:, k, tg * 512:(tg + 1) * 512], start=(k == 0), stop=(k == 15),
                                     w=(waits if k == 0 else []), m=(k == 15))
                    wt_last_mm[gi] = t_mm
                    fs = st["f"] % 2
                    st["f"] += 1
                    func = AF.Silu if kind == "lg" else AF.Copy
                    t_e = kb.op("act", "activation", out=F16[fs][:, :], in_=pb[0:96, 0:512], func=func,
                                w=[t_mm, rF.tok(fs)], m=True)
                    bank_free[u % 3] = t_e
                    nbk = h0 + n
                    if kind == "lx":
                        xr = (nbk // 2) * 576 + 384 + (nbk % 2) * 96
                        dst = XD2[xr:xr + 96, tg * 512:(tg + 1) * 512]
                    else:
                        dst = lgT[nbk, :, tg * 512:(tg + 1) * 512]
                    rF.dma("sp", fs, dst, F16[fs][:, :], w=[t_e])

                for gi in range(ngr):
                    kind = COLG[gi][0]
                    if gi + 1 < ngr:
                        prev = [wt_last_mm[gi - 1]] if gi >= 1 else []
                        wtok[gi + 1] = load_w(gi + 1, prev)
                    if kind in ("lx", "lg"):
                        for n in range(4):
                            for tg in range(4):
                                unit_featmajor(gi, n, tg)
                    else:
                        for tt in range(NTT):
                            if gi == 0:
                                norm_tile(tt)
                            unit_tokmajor(gi, tt)
                kb.barrier([rx, rw, rO, rOT, rF, rg_])

        def phaseR(l):
            with ExitStack() as es:
                def sb(name, shape, dt):
                    return es.enter_context(nc.sbuf_tensor(uniq(name), shape, dt))

                def psb(name, shape, dt):
                    return es.enter_context(nc.psum_tensor(uniq(name), shape, dt))
                kt = [sb(f"kt{i}", [128, 768], BF16) for i in range(2)]
                vt = [sb(f"vt{i}", [128, 768], BF16) for i in range(2)]
                kTt = [sb(f"kTt{i}", [128, 6, 128], BF16) for i in range(2)]
                qTt = [sb(f"qTt{i}", [128, 6, 128], BF16) for i in range(2)]
                gt = [sb(f"gt{i}", [128, 768], BF16) for i in range(2)]
                Sst = sb("Sst", [128, 6, 128], F32)
                Stmp = sb("Stmp", [128, 6, 128], F32)
                Sb16 = sb("Sb16", [128, 6, 128], BF16)
                SG = sb("SG", [128, 4, 768], F32)
                pT = sb("pT", [128, 6, 128], BF16)
                O32 = sb("O32", [128, 768], F32)
                SQ = sb("SQ", [128, 768], F32)
                RN = sb("RN", [128, 768], F32)
                RO = sb("RO", [128, 768], BF16)
                ROT = [sb(f"ROT{i}", [128, 768], BF16) for i in range(2)]
                stt = sb("stt", [128, 6], F32)
                stm = sb("stm", [128, 6], F32)
                stl = sb("stl", [128, 6], F32)
                str_ = sb("str", [128, 6], F32)
                ps_a = [psb(f"psa{i}", [128, 512], F32) for i in range(2)]
                ps_o = [psb(f"pso{i}", [128, 512], F32) for i in range(2)]
                ps_k = [psb(f"psk{i}", [128, 512], F32) for i in range(2)]
                ps_t = psb("pst", [128, 1024], BF16)
                r1 = Ring(kb, f"r1_{l}_", 2)
                r2 = Ring(kb, f"r2_{l}_", 2)
                rs = Ring(kb, f"rrs{l}_", 1)
                rot = Ring(kb, f"rot{l}_", 2)

                def banks(ps, h):
                    return ps[h // 4][:, (h % 4) * 128:(h % 4 + 1) * 128]
                gcB = gcb[:, :].unsqueeze(2).to_broadcast([128, 6, 128])

                def state_update(ps, t_mm, extra_w):
                    kb.op("dve", "tensor_tensor", out=Stmp[:, 0:4, :].rearrange("p h e -> p (h e)"),
                          in0=ps[0][:, :], in1=Sst[:, 0:4, :].rearrange("p h e -> p (h e)"), op=ALU.add,
                          w=[t_mm] + extra_w)
                    ta = kb.op("dve", "tensor_tensor", out=Stmp[:, 4:6, :].rearrange("p h e -> p (h e)"),
                               in0=ps[1][:, 0:256], in1=Sst[:, 4:6, :].rearrange("p h e -> p (h e)"), op=ALU.add, m=True)
                    tb = kb.op("dve", "tensor_tensor", out=Sst[:, :, :], in0=Stmp[:, :, :], in1=gcB, op=ALU.mult,
                               w=[ta], m=True)
                    return ta, tb

                t_z = kb.op("dve", "memset", Sst[:, :, :].rearrange("p h e -> p (h e)"), 0.0, m=True)
                free1 = [[], []]
                t_S = t_z
                kv_free = None
                for n in range(NTT):
                    s = n % 2
                    r1.dma("sp", s, kt[s][:, :], rk[n * 128:(n + 1) * 128, :], w=free1[s])
                    t_l = r1.dma("sp", s, vt[s][:, :], rv[n * 128:(n + 1) * 128, :])
                    for h in range(6):
                        t_mm = kb.op("pe", "matmul", banks(ps_k, h), lhsT=kt[s][:, h * 128:(h + 1) * 128],
                                     rhs=vt[s][:, h * 128:(h + 1) * 128], start=True, stop=True,
                                     w=[t_l, kv_free], m=(h == 5))
                    free1[s] = [t_mm]
                    ta, t_S = state_update(ps_k, t_mm, [t_S])
                    kv_free = ta
                t_rs = rs.dma("sp", 0, RS.ap().rearrange("(h d) e -> d h e", h=6), Sst[:, :, :], w=[t_S])
                t_cc = allgather(RS, RSG[l], [t_rs])
                t_g = rs.dma("sp", 0, SG[:, :, :].rearrange("p r (h e) -> p r h e", h=6),
                             RSG[l].ap().rearrange("(r h d) e -> d r h e", r=4, h=6), w=[t_cc, t_rs])

                def cfB(i):
                    return cfb[:, i * 6:(i + 1) * 6].unsqueeze(2).to_broadcast([128, 6, 128])

                def SGv(i):
                    return SG[:, i, :].rearrange("p (h e) -> p h e", h=6)
                t0 = kb.op("dve", "tensor_tensor", out=Sst[:, :, :], in0=SGv(0), in1=cfB(0), op=ALU.mult,
                           w=[t_g, t_S], m=True)
                for i in (1, 2):
                    t1 = kb.op("dve", "tensor_tensor", out=Stmp[:, :, :], in0=SGv(i), in1=cfB(i), op=ALU.mult,
                               w=[t0], m=True)
                    t0 = kb.op("dve", "tensor_tensor", out=Sst[:, :, :], in0=Sst[:, :, :], in1=Stmp[:, :, :],
                               op=ALU.add, w=[t1], m=True)
                t_S = t0
                kb.barrier([r1, rs])

                free2 = [[], []]
                sc_free = None
                o_free = None
                pT_free = None
                sb_free = None
                rot_free = [None, None]
                t_pst_free = None
                t_o32_free = None
                t_ro_free = None
                maskB = maskT[:, :].unsqueeze(1).to_broadcast([128, 4, 128])
                maskB2 = maskT[:, :].unsqueeze(1).to_broadcast([128, 2, 128])
                for n in range(NTT):
                    s = n % 2
                    c0 = n * 128
                    r2.dma("sp", s, kt[s][:, :], rk[c0:c0 + 128, :], w=free2[s])
                    r2.dma("sp", s, vt[s][:, :], rv[c0:c0 + 128, :])
                    r2.dma("sp", s, gt[s][:, :], rg[c0:c0 + 128, :])
                    r2.dma("sp", s, kTt[s][:, :, :], rkT[:, :, c0:c0 + 128].rearrange("h d t -> d h t"))
                    t_l = r2.dma("sp", s, qTt[s][:, :, :], rqT[:, :, c0:c0 + 128].rearrange("h d t -> d h t"))
                    for h in range(6):
                        t_sc = kb.op("pe", "matmul", banks(ps_a, h), lhsT=kTt[s][:, h, :], rhs=qTt[s][:, h, :],
                                     start=True, stop=True, w=[t_l, sc_free], m=(h == 5))
                    kb.op("dve", "tensor_tensor", out=pT[:, 0:4, :], in0=ps_a[0][:, :].rearrange("p (h i) -> p h i", h=4),
                          in1=maskB, op=ALU.mult, w=[t_sc, pT_free])
                    t_p = kb.op("dve", "tensor_tensor", out=pT[:, 4:6, :],
                                in0=ps_a[1][:, 0:256].rearrange("p (h i) -> p h i", h=2), in1=maskB2, op=ALU.mult, m=True)
                    sc_free = t_p
                    t_sb = kb.op("act", "activation", out=Sb16[:, :, :].rearrange("p h e -> p (h e)"),
                                 in_=Sst[:, :, :].rearrange("p h e -> p (h e)"), func=AF.Copy, w=[t_S, sb_free], m=True)
                    for h in range(6):
                        kb.op("pe", "matmul", banks(ps_o, h), lhsT=pT[:, h, :], rhs=vt[s][:, h * 128:(h + 1) * 128],
                              start=True, stop=False, w=[t_p, o_free])
                        t_o = kb.op("pe", "matmul", banks(ps_o, h), lhsT=qTt[s][:, h, :], rhs=Sb16[:, h, :],
                                    start=False, stop=True, w=[t_sb], m=(h == 5))
                    pT_free = t_o
                    sb_free = t_o
                    for h in range(6):
                        t_kv = kb.op("pe", "matmul", banks(ps_k, h), lhsT=kt[s][:, h * 128:(h + 1) * 128],
                                     rhs=vt[s][:, h * 128:(h + 1) * 128], start=True, stop=True,
                                     w=[kv_free], m=(h == 5))
                    free2[s] = [t_kv, t_o]
                    ta, t_S = state_update(ps_k, t_kv, [t_S, t_sb])
                    kv_free = ta
                    kb.op("act", "activation", out=O32[:, 0:512], in_=ps_o[0][:, :], func=AF.Copy, w=[t_o, t_o32_free])
                    t_e = kb.op("act", "activation", out=O32[:, 512:768], in_=ps_o[1][:, 0:256], func=AF.Copy, m=True)
                    o_free = t_e
                    t_q = kb.op("dve", "tensor_tensor", out=SQ[:, :], in0=O32[:, :], in1=O32[:, :], op=ALU.mult,
                                w=[t_e], m=True)
                    t_r = kb.op("dve", "tensor_reduce", out=stt[:, :], in_=SQ[:, :].rearrange("p (h e) -> p h e", h=6),
                                axis=AX.X, op=ALU.add, w=[t_q], m=True)
                    t_m = kb.op("dve", "tensor_scalar", out=stm[:, :], in0=stt[:, :], scalar1=1.0 / 128, scalar2=EPS,
                                op0=ALU.mult, op1=ALU.add, w=[t_r], m=True)
                    t_ln = kb.op("act", "activation", out=stl[:, :], in_=stm[:, :], func=AF.Ln, w=[t_m], m=True)
                    t_rs_ = kb.op("act", "activation", out=str_[:, :], in_=stl[:, :], func=AF.Exp, scale=-0.5,
                                  w=[t_ln], m=True)
                    t_n = kb.op("dve", "tensor_tensor", out=RN[:, :].rearrange("p (h e) -> p h e", h=6),
                                in0=O32[:, :].rearrange("p (h e) -> p h e", h=6),
                                in1=str_[:, :].unsqueeze(2).to_broadcast([128, 6, 128]), op=ALU.mult, w=[t_rs_], m=True)
                    t_o32_free = t_n
                    t_ro = kb.op("dve", "tensor_tensor", out=RO[:, :], in0=RN[:, :], in1=gt[s][:, :], op=ALU.mult,
                                 w=[t_n, t_ro_free], m=True)
                    free2[s].append(t_ro)
                    for h in range(6):
                        t_tr = kb.op("pe", "transpose", out=ps_t[:, h * 128:(h + 1) * 128],
                                     in_=RO[:, h * 128:(h + 1) * 128], identity=ident[:, :],
                                     w=[t_ro, t_pst_free], m=(h == 5))
                    t_ro_free = t_tr
                    t_ev = kb.op("act", "activation", out=ROT[s][:, :], in_=ps_t[:, 0:768], func=AF.Copy,
                                 w=[t_tr, rot.tok(s)], m=True)
                    t_pst_free = t_ev
                    rot.dma("sp", s, mixr[:, :, c0:c0 + 128].rearrange("h e t -> e h t"),
                            ROT[s][:, :].rearrange("p (h t) -> p h t", h=6), w=[t_ev])
                kb.barrier([r2, rot])

        def phase2(l, t_ag):
            lam_init = 0.8 - 0.6 * math.exp(-0.3 * l)
            with ExitStack() as es:
                def sb(name, shape, dt):
                    return es.enter_context(nc.sbuf_tensor(uniq(name), shape, dt))

                def psb(name, shape, dt):
                    return es.enter_context(nc.psum_tensor(uniq(name), shape, dt))
                XO2 = XO.ap()
                rl = Ring(kb, f"p2l{l}_", 1)
                rst = Ring(kb, f"p2s{l}_", 2)
                lp = sb("lp", [96, 2, 8], F32)
                wab = sb("wab", [96, 2, 2, 96], BF16)
                dl = sb("dl", [128, 4, 64], F32)
                gsb_ = sb("gsb", [128, 128], F32)
                lamt = sb("lamt", [128, 8], F32)
                c1 = sb("c1", [96, 2, 4], F32)
                rl.dma("sp", 0, lp[:, :, :], lru_p[l, :, :, :].rearrange("b c k -> c b k"))
                rl.dma("pool", 0, wab[:, :, :, :], lru_w[l, :, :, :, :].rearrange("b g k j -> k b g j"))
                rl.dma("sp", 0, dl[:, :, :].rearrange("p a b -> p (a b)"),
                       dlam[l, :, :].rearrange("a b -> (a b)").partition_broadcast(128))
                t_p = rl.dma("sp", 0, gsb_[:, :], subg[l, :].partition_broadcast(128))
                kb.op("dve", "tensor_tensor", out=dl[:, 0, :], in0=dl[:, 0, :], in1=dl[:, 1, :], op=ALU.mult, w=[t_p])
                t_a = kb.op("dve", "tensor_tensor", out=dl[:, 2, :], in0=dl[:, 2, :], in1=dl[:, 3, :], op=ALU.mult, m=True)
                kb.op("dve", "tensor_reduce", out=lamt[:, 0:1], in_=dl[:, 0, :], axis=AX.X, op=ALU.add, w=[t_a])
                t_b = kb.op("dve", "tensor_reduce", out=lamt[:, 1:2], in_=dl[:, 2, :], axis=AX.X, op=ALU.add, m=True)
                t_c = kb.op("act", "activation", out=lamt[:, 2:4], in_=lamt[:, 0:2], func=AF.Exp, w=[t_b], m=True)
                t_d = kb.op("dve", "tensor_tensor", out=lamt[:, 4:5], in0=lamt[:, 3:4], in1=lamt[:, 2:3], op=ALU.subtract,
                            w=[t_c], m=True)
                t_nl = kb.op("dve", "tensor_scalar", out=lamt[:, 5:6], in0=lamt[:, 4:5], scalar1=-lam_init, scalar2=None,
                             op0=ALU.add, w=[t_d], m=True)
                t_gs = kb.op("dve", "tensor_scalar", out=gsb_[:, :], in0=gsb_[:, :], scalar1=float(1.0 - lam_init),
                             scalar2=None, op0=ALU.mult, w=[t_p], m=True)
                t_e1 = kb.op("act", "activation", out=c1[:, :, 0:1], in_=lp[:, :, 7:8], func=AF.Exp, scale=-1.0,
                             w=[t_p], m=True)
                t_e2 = kb.op("act", "activation", out=c1[:, :, 1:2], in_=c1[:, :, 0:1], func=AF.Ln, bias=1.0,
                             w=[t_e1], m=True)
                kb.op("dve", "tensor_scalar", out=c1[:, :, 2:3], in0=c1[:, :, 1:2], scalar1=-8.0, scalar2=None,
                      op0=ALU.mult, w=[t_e2])
                t_c1 = kb.op("dve", "tensor_scalar", out=c1[:, :, 3:4], in0=c1[:, :, 1:2], scalar1=-16.0, scalar2=None,
                             op0=ALU.mult, m=True)

                CH = 2048
                xp = [sb(f"xp{i}", [96, CH + 3], BF16) for i in range(2)]
                xc = sb("xc", [96, CH], F32)
                xcb = sb("xcb", [96, CH], BF16)
                rr = sb("rr", [96, CH], F32)
                ii = sb("ii", [96, CH], F32)
                aa = sb("aa", [96, CH], F32)
                a2 = sb("a2", [96, CH], F32)
                bb = sb("bb", [96, CH], F32)
                hh_ = sb("hh", [96, CH], F32)
                hb16 = [sb(f"hb16_{i}", [96, CH], BF16) for i in range(2)]
                hprev = sb("hprev", [96, 2], F32)
                pg = [psb(f"pg{i}", [128, 512], F32) for i in range(4)]
                rxp = Ring(kb, f"rxp{l}_", 2)
                t_hp = kb.op("dve", "memset", hprev[:, :], 0.0, m=True)
                pg_free = [None] * 4
                xp_free = [[], []]
                cnt = 0
                t_h_prev = {0: t_hp, 1: t_hp}
                t_scan_prev = None
                hb_free = [None, None]
                t_misc_free = []
                for b in range(2):
                    for s in range(4):
                        xs = cnt % 2
                        cnt += 1
                        t_x = rxp.dma("sp", xs, xp[xs][:, 3:3 + CH], XDL[s, 384 + b * 96:384 + (b + 1) * 96, :],
                                      w=[t_ag] + xp_free[xs])
                        if s == 0:
                            t_halo = kb.op("dve", "memset", xp[xs][:, 0:3], 0.0, w=xp_free[xs], m=True)
                        else:
                            t_halo = kb.op("dve", "tensor_copy", out=xp[xs][:, 0:3], in_=xp[1 - xs][:, CH:CH + 3],
                                           w=xp_free[xs] + [t_xprev], m=True)
                        t_xprev = t_x
                        t_cv = kb.op("dve", "tensor_scalar", out=xc[:, :], in0=xp[xs][:, 0:CH], scalar1=lp[:, b, 0:1],
                                     scalar2=lp[:, b, 4:5], op0=ALU.mult, op1=ALU.add,
                                     w=[t_x, t_halo, t_p] + t_misc_free, m=True)
                        for j in (1, 2, 3):
                            t_cv = kb.op("dve", "scalar_tensor_tensor", out=xc[:, :], in0=xp[xs][:, j:j + CH],
                                         scalar=lp[:, b, j:j + 1], in1=xc[:, :], op0=ALU.mult, op1=ALU.add,
                                         w=[t_cv], m=True)
                        xp_free[xs] = [t_cv]
                        t_cb = kb.op("dve", "tensor_copy", out=xcb[:, :], in_=xc[:, :], w=[t_cv], m=True)
                        toks = []
                        for tg in range(4):
                            cs = slice(tg * 512, (tg + 1) * 512)
                            t_m1 = kb.op("pe", "matmul", pg[0 + (tg % 2) * 2][0:96, :], lhsT=wab[:, b, 0, :], rhs=xcb[:, cs],
                                         start=True, stop=True, w=[t_cb, rl.tok(0), pg_free[(tg % 2) * 2]], m=True)
                            t_m2 = kb.op("pe", "matmul", pg[1 + (tg % 2) * 2][0:96, :], lhsT=wab[:, b, 1, :], rhs=xcb[:, cs],
                                         start=True, stop=True, w=[pg_free[1 + (tg % 2) * 2]], m=True)
                            t_r = kb.op("act", "activation", out=rr[:, cs], in_=pg[0 + (tg % 2) * 2][0:96, :],
                                        func=AF.Sigmoid, bias=lp[:, b, 5:6], w=[t_m1], m=True)
                            pg_free[(tg % 2) * 2] = t_r
                            t_i = kb.op("act", "activation", out=ii[:, cs], in_=pg[1 + (tg % 2) * 2][0:96, :],
                                        func=AF.Sigmoid, bias=lp[:, b, 6:7], w=[t_m2], m=True)
                            pg_free[1 + (tg % 2) * 2] = t_i
                            toks.append(t_i)
                        t_a1 = kb.op("act", "activation", out=aa[:, :], in_=rr[:, :], func=AF.Exp, scale=c1[:, b, 2:3],
                                     w=[t_c1, toks[-1]], m=True)
                        t_a2 = kb.op("act", "activation", out=a2[:, :], in_=rr[:, :], func=AF.Exp, scale=c1[:, b, 3:4],
                                     m=True)
                        t_om = kb.op("dve", "tensor_scalar", out=a2[:, :], in0=a2[:, :], scalar1=-1.0, scalar2=1.0,
                                     op0=ALU.mult, op1=ALU.add, w=[t_a2], m=True)
                        t_l1 = kb.op("act", "activation", out=a2[:, :], in_=a2[:, :], func=AF.Ln, w=[t_om], m=True)
                        t_l2 = kb.op("act", "activation", out=a2[:, :], in_=a2[:, :], func=AF.Exp, scale=0.5,
                                     w=[t_l1], m=True)
                        t_b1 = kb.op("dve", "tensor_tensor", out=bb[:, :], in0=ii[:, :], in1=xc[:, :], op=ALU.mult,
                                     w=[toks[-1], t_scan_prev], m=True)
                        t_b2 = kb.op("dve", "tensor_tensor", out=bb[:, :], in0=bb[:, :], in1=a2[:, :], op=ALU.mult,
                                     w=[t_b1, t_l2], m=True)
                        t_sc = kb.op("dve", "tensor_tensor_scan", out=hh_[:, :], data0=aa[:, :], data1=bb[:, :],
                                     initial=hprev[:, b:b + 1], op0=ALU.mult, op1=ALU.add,
                                     w=[t_b2, t_a1, t_h_prev[b]], m=True)
                        t_scan_prev = t_sc
                        t_h_prev[b] = kb.op("dve", "tensor_copy", out=hprev[:, b:b + 1], in_=hh_[:, CH - 1:CH],
                                            w=[t_sc], m=True)
                        hs = (cnt - 1) % 2
                        t_hb = kb.op("act", "activation", out=hb16[hs][:, :], in_=hh_[:, :], func=AF.Copy,
                                     w=[t_sc, rst.tok(hs)], m=True)
                        t_misc_free = [t_hb, t_b2, t_l2]
                        r0 = s * 320 + 128 + b * 96
                        rst.dma("sp", hs, XO2[r0:r0 + 96, :], hb16[hs][:, :], w=[t_hb])
                kb.barrier([rxp, rst, rl])

            with ExitStack() as es:
                def sb(name, shape, dt):
                    return es.enter_context(nc.sbuf_tensor(uniq(name), shape, dt))

                def psb(name, shape, dt):
                    return es.enter_context(nc.psum_tensor(uniq(name), shape, dt))
                XO2 = XO.ap()
                qT0 = sb("qT0", [128, S], BF16)
                qT1 = sb("qT1", [128, S], BF16)
                kT = sb("kT", [128, S], BF16)
                v1 = sb("v1", [128, 64, 132], BF16)
                Pm = [sb(f"Pm{i}", [128, 2, 512], BF16) for i in range(2)]
                accS = sb("accS", [128, 8, 132], F32)
                o1 = sb("o1", [128, 128], F32)
                oo = sb("oo", [128, 128], F32)
                sqj = sb("sqj", [128, 128], F32)
                on16 = [sb(f"on16_{i}", [128, 128], BF16) for i in range(2)]
                sm = sb("sm", [128, 16], F32)
                gsb2 = sb("gsb2", [128, 128], F32)
                lam2 = sb("lam2", [128, 2], F32)
                Sps = [[psb(f"Sps{i}_{c}", [128, 512], F32) for c in range(2)] for i in range(2)]
                accp = [psb(f"accp{i}", [128, 512], F32) for i in range(3)]
                ra = Ring(kb, f"ra{l}_", 1)
                ro = Ring(kb, f"ro{l}_", 2)
                lam_init = 0.8 - 0.6 * math.exp(-0.3 * l)
                dl = sb("dl2", [128, 4, 64], F32)
                lamt = sb("lamt2", [128, 8], F32)
                ra.dma("sp", 0, dl[:, :, :].rearrange("p a b -> p (a b)"),
                       dlam[l, :, :].rearrange("a b -> (a b)").partition_broadcast(128))
                t_p = ra.dma("sp", 0, gsb2[:, :], subg[l, :].partition_broadcast(128))
                kb.op("dve", "tensor_tensor", out=dl[:, 0, :], in0=dl[:, 0, :], in1=dl[:, 1, :], op=ALU.mult, w=[t_p])
                t_a = kb.op("dve", "tensor_tensor", out=dl[:, 2, :], in0=dl[:, 2, :], in1=dl[:, 3, :], op=ALU.mult, m=True)
                kb.op("dve", "tensor_reduce", out=lamt[:, 0:1], in_=dl[:, 0, :], axis=AX.X, op=ALU.add, w=[t_a])
                t_b = kb.op("dve", "tensor_reduce", out=lamt[:, 1:2], in_=dl[:, 2, :], axis=AX.X, op=ALU.add, m=True)
                t_c = kb.op("act", "activation", out=lamt[:, 2:4], in_=lamt[:, 0:2], func=AF.Exp, w=[t_b], m=True)
                t_d = kb.op("dve", "tensor_tensor", out=lamt[:, 4:5], in0=lamt[:, 3:4], in1=lamt[:, 2:3], op=ALU.subtract,
                            w=[t_c], m=True)
                t_nl = kb.op("dve", "tensor_scalar", out=lam2[:, 0:1], in0=lamt[:, 4:5], scalar1=-lam_init, scalar2=None,
                             op0=ALU.add, w=[t_d], m=True)
                t_gs = kb.op("dve", "tensor_scalar", out=gsb2[:, :], in0=gsb2[:, :], scalar1=float(1.0 - lam_init),
                             scalar2=None, op0=ALU.mult, w=[t_p], m=True)
                kb.op("pool", "memset", qT0[64:128, :], 0.0)
                kb.op("pool", "memset", qT1[0:64, :], 0.0)
                t_ms = kb.op("pool", "memset", v1[:, :, 128:132], 1.0, m=True)
                for s in range(4):
                    cs = slice(s * 2048, (s + 1) * 2048)
                    ra.dma("sp", 0, qT0[0:64, cs], XDL[s, 0:64, :], w=[t_ag])
                    ra.dma("sp", 0, qT1[64:128, cs], XDL[s, 64:128, :])
                    ra.dma("sp", 0, kT[:, cs], XDL[s, 128:256, :])
                    t_ld = ra.dma("sp", 0, v1[:, s * 16:(s + 1) * 16, 0:128],
                                  XDL[s, 256:384, :].rearrange("r (a e) -> (r a) e", e=128).rearrange(
                                      "(a p) e -> p a e", p=128))
                pairs = []
                for G in range(16):
                    for j in range(4 * G + 4):
                        pairs.append((G, j))
                NP = len(pairs)
                s_free = [None, None]
                p_free = [None, None]
                t_qk = [None] * NP
                acc_free = None
                on_cnt = 0

                def slot_ap(c, q4, lo, hi):
                    idx = c * 4 + q4
                    return accp[idx // 3][:, (idx % 3) * 160 + lo:(idx % 3) * 160 + hi]

                def emit_qk(u):
                    G, j = pairs[u]
                    bf = u % 2
                    a = max(0, j - 4 * G)
                    c0 = a * 128
                    for c in range(2):
                        qs = qT0 if c == 0 else qT1
                        t_ = kb.op("pe", "matmul", Sps[bf][c][:, c0:512], lhsT=kT[:, j * 128:(j + 1) * 128],
                                   rhs=qs[:, G * 512 + c0:(G + 1) * 512], start=True, stop=True,
                                   w=[t_ld, t_ms, s_free[bf]], m=(c == 1))
                    t_qk[u] = t_

                emit_qk(0)
                for u in range(NP):
                    G, j = pairs[u]
                    bf = u % 2
                    a = max(0, j - 4 * G)
                    diag = j >= 4 * G
                    c0 = a * 128
                    if u + 1 < NP:
                        emit_qk(u + 1)
                    for c in range(2):
                        t_ex = kb.op("act", "activation", out=Pm[bf][:, c, c0:512], in_=Sps[bf][c][:, c0:512],
                                     func=AF.Exp, scale=0.125, w=[t_qk[u], p_free[bf]], m=(c == 1))
                    s_free[bf] = t_ex
                    t_pv_in = t_ex
                    if diag:
                        t_pv_in = kb.op("dve", "tensor_tensor", out=Pm[bf][:, :, c0:c0 + 128],
                                        in0=Pm[bf][:, :, c0:c0 + 128],
                                        in1=maskT[:, :].unsqueeze(1).to_broadcast([128, 2, 128]), op=ALU.mult,
                                        w=[t_ex], m=True)
                    first = (j == 0)
                    for c in range(2):
                        for q4 in range(a, 4):
                            idx = c * 4 + q4
                            st_flag = first and (idx % 3 == 0)
                            last = (c == 1 and q4 == 3)
                            t_pv = kb.op("pe", "matmul", slot_ap(c, q4, 0, 129), lhsT=Pm[bf][:, c, q4 * 128:(q4 + 1) * 128],
                                         rhs=v1[:, j, 0:129], start=st_flag, stop=(j == 4 * G + q4),
                                         skip_group_check=True,
                                         w=([t_pv_in, acc_free] if (c == 0 and q4 == a) else []), m=last)
                    p_free[bf] = t_pv
                    if j == 4 * G + 3:
                        for idx in range(8):
                            c, q4 = idx // 4, idx % 4
                            t_ev = kb.op("act", "activation", out=accS[:, idx, 0:129], in_=slot_ap(c, q4, 0, 129),
                                         func=AF.Copy, w=[t_pv], m=(idx == 7))
                        acc_free = t_ev
                        t_prev = t_ev
                        for q4 in range(4):
                            t1 = kb.op("dve", "reciprocal", out=sm[:, 0:1], in_=accS[:, q4, 128:129], w=[t_prev], m=True)
                            t2 = kb.op("dve", "reciprocal", out=sm[:, 1:2], in_=accS[:, 4 + q4, 128:129], m=True)
                            t3 = kb.op("dve", "tensor_tensor", out=sm[:, 2:3], in0=sm[:, 1:2], in1=lam2[:, 0:1],
                                       op=ALU.mult, w=[t2, t_nl], m=True)
                            t4 = kb.op("dve", "tensor_scalar", out=o1[:, :], in0=accS[:, q4, 0:128], scalar1=sm[:, 0:1],
                                       scalar2=None, op0=ALU.mult, w=[t1], m=True)
                            t5 = kb.op("dve", "scalar_tensor_tensor", out=oo[:, :], in0=accS[:, 4 + q4, 0:128],
                                       scalar=sm[:, 2:3], in1=o1[:, :], op0=ALU.mult, op1=ALU.add, w=[t3, t4], m=True)
                            t6 = kb.op("act", "activation", out=sqj[:, :], in_=oo[:, :], func=AF.Square,
                                       accum_out=sm[:, 3:4], w=[t5], m=True)
                            t7 = kb.op("dve", "tensor_scalar", out=sm[:, 4:5], in0=sm[:, 3:4], scalar1=1.0 / 128,
                                       scalar2=EPS, op0=ALU.mult, op1=ALU.add, w=[t6], m=True)
                            t8 = kb.op("act", "activation", out=sm[:, 5:6], in_=sm[:, 4:5], func=AF.Ln, w=[t7], m=True)
                            t9 = kb.op("act", "activation", out=sm[:, 6:7], in_=sm[:, 5:6], func=AF.Exp, scale=-0.5,
                                       w=[t8], m=True)
                            os_ = on_cnt % 2
                            on_cnt += 1
                            t10 = kb.op("dve", "scalar_tensor_tensor", out=on16[os_][:, :], in0=oo[:, :], scalar=sm[:, 6:7],
                                        in1=gsb2[:, :], op0=ALU.mult, op1=ALU.mult, w=[t9, t_gs, ro.tok(os_)], m=True)
                            t_prev = t10
                            sq = G // 4
                            tile_in_q = (G % 4) * 4 + q4
                            r0 = sq * 320 + tile_in_q * 8
                            ro.dma("sp", os_, XO2[r0:r0 + 8, :].rearrange("r (a e) -> (r a) e", e=128), on16[os_][:, :],
                                   w=[t10])
                kb.barrier([ra, ro])

        def phase3(l, x_src, x_dst, t_ag2):
            with ExitStack() as es:
                def sb(name, shape, dt):
                    return es.enter_context(nc.sbuf_tensor(uniq(name), shape, dt))

                def psb(name, shape, dt):
                    return es.enter_context(nc.psum_tensor(uniq(name), shape, dt))
                woA = sb("woA", [128, 10, D], BF16)
                woB = sb("woB", [96, 8, D], BF16)
                gpb = sb("gpb", [128, D], F32)
                mret = [sb(f"mret{i}", [128, 6, 128], BF16) for i in range(2)]
                dot = [sb(f"dot{i}", [128, 4, 128], BF16) for i in range(2)]
                dgt = [sb(f"dgt{i}", [128, 512], BF16) for i in range(2)]
                hTt = [sb(f"hTt{i}", [96, 8, 128], BF16) for i in range(2)]
                lgt = [sb(f"lgt{i}", [96, 8, 128], BF16) for i in range(2)]
                xt = [sb(f"x3t{i}", [128, D], F32) for i in range(2)]
                dmix = sb("dmix", [128, 512], BF16)
                dTt = sb("dTt", [128, 4, 128], BF16)
                lmix = sb("lmix", [96, 8, 128], BF16)
                yn = [sb(f"yn{i}", [128, D], F32) for i in range(2)]
                jk = sb("jk3", [128, 512], BF16)
                s4 = sb("s4", [128, 8], F32)
                py = [psb(f"py{i}", [128, 512], F32) for i in range(4)]
                pt3 = psb("pt3", [128, 1024], BF16)
                rw3 = Ring(kb, f"rw3{l}_", 1)
                rl3 = Ring(kb, f"rl3{l}_", 2)
                rs3 = Ring(kb, f"rs3{l}_", 2)
                rw3.dma("pool", 0, woA[:, :, :], WO[l][0:1280, :].rearrange("(k p) c -> p k c", p=128), w=[t_wo[l]])
                rw3.dma("pool", 0, woB[:, :, :], WO[l][1280:2048, :].rearrange("(k p) c -> p k c", p=96))
                t_w = rw3.dma("sp", 0, gpb[:, :], post_g[l, :].partition_broadcast(128))
                lfree = [[], []]
                py_free = None
                dmix_free = None
                pt3_free = None
                dT_free = None
                lmix_free = None
                for tt in range(NTT):
                    s = tt % 2
                    c0 = tt * 128
                    rl3.dma("sp", s, mret[s][:, :, :], mixr[:, :, c0:c0 + 128].rearrange("h e t -> e h t"),
                            w=lfree[s] + [t_ag2])
                    for hs in range(4):
                        rl3.dma("sp", s, dot[s][:, hs, :],
                                XOL[hs, tt * 8:(tt + 1) * 8, :].rearrange("r (a e) -> (r a) e", e=128))
                        for b in range(2):
                            rl3.dma("sp", s, hTt[s][:, hs * 2 + b, :], XOL[hs, 128 + b * 96:128 + (b + 1) * 96, c0:c0 + 128])
                    rl3.dma("sp", s, dgt[s][:, :], dg[c0:c0 + 128, :])
                    rl3.dma("sp", s, lgt[s][:, :, :], lgT[:, :, c0:c0 + 128].rearrange("n c t -> c n t"))
                    t_l = rl3.dma("sp", s, xt[s][:, :], x_src[c0:c0 + 128, :])
                    t_dm = kb.op("dve", "tensor_tensor", out=dmix[:, :], in0=dot[s][:, :, :].rearrange("p h e -> p (h e)"),
                                 in1=dgt[s][:, :], op=ALU.mult, w=[t_l, dmix_free], m=True)
                    for hs in range(4):
                        t_tr = kb.op("pe", "transpose", out=pt3[:, hs * 128:(hs + 1) * 128],
                                     in_=dmix[:, hs * 128:(hs + 1) * 128], identity=ident[:, :],
                                     w=[t_dm, pt3_free], m=(hs == 3))
                    dmix_free = t_tr
                    t_dT = kb.op("act", "activation", out=dTt[:, :, :].rearrange("p h t -> p (h t)"), in_=pt3[:, 0:512],
                                 func=AF.Copy, w=[t_tr, dT_free], m=True)
                    pt3_free = t_dT
                    t_lm = kb.op("dve", "tensor_tensor", out=lmix[:, :, :], in0=hTt[s][:, :, :], in1=lgt[s][:, :, :],
                                 op=ALU.mult, w=[t_l, lmix_free], m=True)
                    for cg in range(4):
                        cs = slice(cg * 512, (cg + 1) * 512)
                        first = True
                        for h in range(6):
                            kb.op("pe", "matmul", py[cg][:, :], lhsT=mret[s][:, h, :], rhs=woA[:, h, cs], start=first,
                                  stop=False, w=([t_l, t_w, rw3.tok(0), py_free] if first else []))
                            first = False
                        for h in range(4):
                            kb.op("pe", "matmul", py[cg][:, :], lhsT=dTt[:, h, :], rhs=woA[:, 6 + h, cs], start=False,
                                  stop=False, w=[t_dT])
                        for nb in range(8):
                            t_mm = kb.op("pe", "matmul", py[cg][:, :], lhsT=lmix[:, nb, :], rhs=woB[:, nb, cs],
                                         start=False, stop=(nb == 7), w=[t_lm], m=(nb == 7 and cg == 3))
                    dT_free = t_mm
                    lmix_free = t_mm
                    for cg in range(4):
                        t_sq = kb.op("act", "activation", out=jk[:, :], in_=py[cg][:, :], func=AF.Square,
                                     accum_out=s4[:, cg:cg + 1], w=[t_mm], m=True)
                    t_r = kb.op("dve", "tensor_reduce", out=s4[:, 4:5], in_=s4[:, 0:4], axis=AX.X, op=ALU.add,
                                w=[t_sq], m=True)
                    t_m = kb.op("dve", "tensor_scalar", out=s4[:, 5:6], in0=s4[:, 4:5], scalar1=1.0 / D, scalar2=EPS,
                                op0=ALU.mult, op1=ALU.add, w=[t_r], m=True)
                    t_ln = kb.op("act", "activation", out=s4[:, 6:7], in_=s4[:, 5:6], func=AF.Ln, w=[t_m], m=True)
                    t_rs = kb.op("act", "activation", out=s4[:, 7:8], in_=s4[:, 6:7], func=AF.Exp, scale=-0.5,
                                 w=[t_ln], m=True)
                    for cg in range(4):
                        cs = slice(cg * 512, (cg + 1) * 512)
                        t_y = kb.op("dve", "scalar_tensor_tensor", out=yn[s][:, cs], in0=py[cg][:, :], scalar=s4[:, 7:8],
                                    in1=gpb[:, cs], op0=ALU.mult, op1=ALU.mult, w=[t_rs, rs3.tok(s)], m=True)
                    py_free = t_y
                    t_o = kb.op("dve", "tensor_tensor", out=yn[s][:, :], in0=yn[s][:, :], in1=xt[s][:, :], op=ALU.add,
                                w=[t_y], m=True)
                    lfree[s] = [t_o, t_mm, t_lm, t_dm]
                    rs3.dma("sp", s, x_dst[c0:c0 + 128, :], yn[s][:, :], w=[t_o])
                kb.barrier([rw3, rl3, rs3])

        for l in range(nlayers):
            x_src = x_in if l == 0 else x1
            x_dst = x1 if l == 0 and nlayers > 1 else out_t
            phase1(l, x_src)
            if stop_after == ("p1", l):
                break
            t_ag1 = allgather(XD, XDG[l], [], groups=ALL8)
            t_ag1 = copy_local(XDG[l], XDL, 576, t_ag1)
            phaseR(l)
            if stop_after == ("r", l):
                break
            phase2(l, t_ag1)
            if stop_after == ("p2", l):
                break
            t_ag2 = allgather(XO, XOG[l], [], groups=ALL8)
            t_ag2 = copy_local(XOG[l], XOL, 320, t_ag2)
            phase3(l, x_src, x_dst, t_ag2)
        if debug:
            fin.dma("sp", 0, dbgs["XDd"][:, :], XD[:, :])
            fin.dma("sp", 0, dbgs["XOd"][:, :], XO[:, :])
        kb.barrier([fin])
    return nc


def _consts():
    theta = np.float32(10000.0)
    inv_r = (np.float32(1.0) / (theta ** np.linspace(0.0, 1.0, 64, dtype=np.float32))).astype(np.float32)
    inv_d = (np.float32(1.0) / (theta ** (np.arange(0, 64, 2, dtype=np.float32) / np.float32(64)))).astype(np.float32)
    invf = np.concatenate([inv_r, inv_d]).astype(np.float32)
    hh = np.arange(6, dtype=np.float64)
    log_g = np.log1p(-np.power(2.0, -5.0 - hh))
    p1 = np.arange(1, 129, dtype=np.float64)[:, None]
    dq = np.exp(p1 * log_g[None, :])
    dk = np.exp(-p1 * log_g[None, :]) * (128.0 ** -0.5)
    dec = np.concatenate([dq, dk], axis=1).astype(np.float32)
    gc = np.exp(128.0 * log_g).astype(np.float32)
    gR = np.exp(2048.0 * log_g)
    cfs = []
    for r in range(4):
        cf = np.zeros((3, 6), np.float64)
        for i in range(3):
            if i < r:
                cf[i] = gR ** (r - 1 - i)
        cfs.append(cf.reshape(-1).astype(np.float32))
    ident = np.eye(128, dtype=np.float32).astype(ml_dtypes.bfloat16)
    jj = np.arange(128)[:, None]
    ii = np.arange(128)[None, :]
    mask = (jj <= ii).astype(np.float32).astype(ml_dtypes.bfloat16)
    return invf, dec, gc, cfs, ident, mask


def make_in_maps(inputs):
    invf, dec, gc, cfs, ident, mask = _consts()
    x = np.asarray(inputs["x"], dtype=np.float32)
    pos = np.asarray(inputs["positions"]).astype(np.int32)
    w_in = np.asarray(inputs["w_in"], dtype=np.float32)
    w_out = np.asarray(inputs["w_out"], dtype=np.float32)
    RW = D // NCORES
    pre_g = np.ascontiguousarray(np.asarray(inputs["pre_norm_g"], dtype=np.float32))
    post_g = np.ascontiguousarray(np.asarray(inputs["post_norm_g"], dtype=np.float32))
    dlam = np.stack([np.asarray(inputs[k], dtype=np.float32) for k in
                     ("diff_lambda_q1", "diff_lambda_k1", "diff_lambda_q2", "diff_lambda_k2")], axis=1)
    subg = np.ascontiguousarray(np.asarray(inputs["diff_subln_g"], dtype=np.float32))
    conv_w = np.asarray(inputs["lru_conv_w"], dtype=np.float32)
    conv_b = np.asarray(inputs["lru_conv_b"], dtype=np.float32)
    wa = np.asarray(inputs["lru_wa"], dtype=np.float32)
    wx = np.asarray(inputs["lru_wx"], dtype=np.float32)
    ba = np.asarray(inputs["lru_ba"], dtype=np.float32)
    bx = np.asarray(inputs["lru_bx"], dtype=np.float32)
    lam = np.asarray(inputs["lru_lambda"], dtype=np.float32)
    in_maps = []
    for c in range(NCORES):
        b, r = c // 4, c % 4
        xs = np.ascontiguousarray(x[b, r * T:(r + 1) * T, :])
        ps = np.ascontiguousarray(pos[b, r * T:(r + 1) * T].reshape(NTT, 128).T)
        lw = np.zeros((L, 2, 2, 96, 96), np.float32)
        lp = np.zeros((L, 2, 96, 8), np.float32)
        for bb in range(2):
            nb = 2 * r + bb
            sl = slice(nb * 96, (nb + 1) * 96)
            lw[:, bb, 0] = wa[:, nb]
            lw[:, bb, 1] = wx[:, nb]
            lp[:, bb, :, 0:4] = np.transpose(conv_w[:, :, sl], (0, 2, 1))
            lp[:, bb, :, 4] = conv_b[:, sl]
            lp[:, bb, :, 5] = ba[:, sl]
            lp[:, bb, :, 6] = bx[:, sl]
            lp[:, bb, :, 7] = lam[:, sl]
        in_maps.append({
            "x_c": xs, "pos_c": ps,
            "w_in_s": np.ascontiguousarray(w_in[:, c * RW:(c + 1) * RW, :]),
            "w_out_s": np.ascontiguousarray(w_out[:, c * RW:(c + 1) * RW, :]), "pre_g": pre_g, "post_g": post_g,
            "dlam": np.ascontiguousarray(dlam), "subg": subg, "lru_w": lw, "lru_p": lp,
            "c_ident": ident, "c_mask": mask, "c_invf": invf, "c_dec": dec, "c_gc": gc, "c_cf": cfs[r],
        })
    return in_maps


_NC_CACHE = {}


def kernel(**inputs):
    if "nc" not in _NC_CACHE:
        _NC_CACHE["nc"] = build()
    nc = _NC_CACHE["nc"]
    in_maps = make_in_maps(inputs)
    res = run_bass_kernel_spmd(nc, in_maps, core_ids=list(range(NCORES)))
    out = np.empty((2, S, D), np.float32)
    for c in range(NCORES):
        b, r = c // 4, c % 4
        out[b, r * T:(r + 1) * T, :] = res.results[c]["out"]
    return out
```

```python
import math
from contextlib import ExitStack

import numpy as np
import ml_dtypes

import concourse.bass as bass
import concourse.mybir as mybir
from concourse.bass_utils import run_bass_kernel_spmd

F32 = mybir.dt.float32
BF16 = mybir.dt.bfloat16
I32 = mybir.dt.int32
AF = mybir.ActivationFunctionType
ALU = mybir.AluOpType
AX = mybir.AxisListType

NCORES = 8
NR = 4
S = 8192
D = 2048
T = 2048
NTT = 16
L = 2
DIN = 6656
EPS = 1e-6
XD_ROWS = 2304
XO_ROWS = 1280
GROUPS = [[0, 1, 2, 3], [4, 5, 6, 7]]

COLG = [
    ("dq", 3072, 512, 0), ("dk", 3584, 512, 0), ("dv", 4096, 512, 0),
    ("lx", 5120, 384, 0), ("lx", 5504, 384, 4),
    ("dg", 4608, 512, 0),
    ("lg", 5888, 384, 0), ("lg", 6272, 384, 4),
    ("rk", 768, 512, 0), ("rk", 1280, 256, 4),
    ("rv", 1536, 512, 0), ("rv", 2048, 256, 4),
    ("rq", 0, 512, 0), ("rq", 512, 256, 4),
    ("rg", 2304, 512, 0), ("rg", 2816, 256, 4),
]
N_XD_GROUPS = 5
WCH = [(3072, 1536), (5120, 768), (4608, 512), (5888, 768), (0, 1536), (1536, 1536)]


def wchunk(c0):
    for i, (a, w) in enumerate(WCH):
        if a <= c0 < a + w:
            return i, c0 - a
    raise ValueError(c0)


class KB:
    def __init__(self, nc):
        self.nc = nc
        self.eng = dict(pe=nc.tensor, act=nc.scalar, dve=nc.vector, pool=nc.gpsimd, sp=nc.sync)
        self.sem = {e: nc.alloc_semaphore("sem_" + e) for e in ("pe", "act", "dve", "pool")}
        self.cnt = {e: 0 for e in self.sem}
        self.last = {e: (None, None) for e in self.sem}
        self.waited = {e: {} for e in self.eng}

    def op(self, e, name, *args, w=(), m=False, **kw):
        for t in w:
            self.wait(e, t)
        ins = getattr(self.eng[e], name)(*args, **kw)
        tok = None
        if m:
            self.cnt[e] += 1
            ins.then_inc(self.sem[e], 1)
            tok = ("c", e, self.cnt[e])
        self.last[e] = (ins, tok)
        return tok

    def mark_last(self, e):
        ins, tok = self.last[e]
        if ins is None:
            return None
        if tok is None:
            self.cnt[e] += 1
            ins.then_inc(self.sem[e], 1)
            tok = ("c", e, self.cnt[e])
            self.last[e] = (ins, tok)
        return tok

    def wait(self, e, t):
        if t is None:
            return
        if isinstance(t, list):
            for tt in t:
                self.wait(e, tt)
            return
        if t[0] == "c":
            key, val, sem = t[1], t[2], self.sem[t[1]]
        else:
            key, val, sem = t[3], t[2], t[1]
        if self.waited[e].get(key, 0) >= val:
            return
        self.waited[e][key] = val
        self.eng[e].wait_ge(sem, val)

    def barrier(self, rings=()):
        toks = [self.mark_last(e) for e in ("pe", "act", "dve", "pool")]
        for r in rings:
            toks += r.all_toks()
        for e in self.eng:
            for t in toks:
                self.wait(e, t)


class Ring:
    def __init__(self, kb, name, n):
        self.kb = kb
        self.name = name
        self.n = n
        self.sems = [kb.nc.alloc_semaphore(f"{name}{i}") for i in range(n)]
        self.cnt = [0] * n

    def dma(self, e, slot, out, in_, w=(), **kw):
        for t in w:
            self.kb.wait(e, t)
        ins = self.kb.eng[e].dma_start(out=out, in_=in_, **kw)
        self.cnt[slot] += 16
        ins.then_inc(self.sems[slot], 16)
        return self.tok(slot)

    def tok(self, slot):
        if self.cnt[slot] == 0:
            return None
        return ("d", self.sems[slot], self.cnt[slot], (self.name, slot))

    def all_toks(self):
        return [self.tok(s) for s in range(self.n)]


def build(nlayers=L, stop_after=None, debug=False):
    nc = bass.Bass("TRN2", target_bir_lowering=False)
    kb = KB(nc)
    dbg_kind = "ExternalOutput" if debug else "Internal"

    ucnt = [0]

    def uniq(name):
        ucnt[0] += 1
        return f"{name}_u{ucnt[0]}"

    def din(name, shape, dt):
        return nc.dram_tensor(name, shape, dt, kind="ExternalInput")

    x_in = din("x_c", [T, D], F32)
    pos_in = din("pos_c", [128, NTT], I32)
    w_in_s = din("w_in_s", [L, D // NCORES, DIN], F32)
    w_out_s = din("w_out_s", [L, D // NCORES, D], F32)
    pre_g = din("pre_g", [L, D], F32)
    post_g = din("post_g", [L, D], F32)
    dlam = din("dlam", [L, 4, 64], F32)
    subg = din("subg", [L, 128], F32)
    lru_w = din("lru_w", [L, 2, 2, 96, 96], F32)
    lru_p = din("lru_p", [L, 2, 96, 8], F32)
    c_ident = din("c_ident", [128, 128], BF16)
    c_mask = din("c_mask", [128, 128], BF16)
    c_invf = din("c_invf", [96], F32)
    c_dec = din("c_dec", [128, 12], F32)
    c_gc = din("c_gc", [6], F32)
    c_cf = din("c_cf", [18], F32)
    out_t = nc.dram_tensor("out", [T, D], F32, kind="ExternalOutput")

    x1 = nc.dram_tensor("x1", [T, D], F32, kind=dbg_kind)
    rqT = nc.dram_tensor("rqT", [6, 128, T], BF16, kind=dbg_kind)
    rkT = nc.dram_tensor("rkT", [6, 128, T], BF16, kind=dbg_kind)
    rk = nc.dram_tensor("rk", [T, 768], BF16, kind=dbg_kind)
    rv = nc.dram_tensor("rv", [T, 768], BF16, kind=dbg_kind)
    rg = nc.dram_tensor("rg", [T, 768], BF16, kind=dbg_kind)
    dg = nc.dram_tensor("dg", [T, 512], BF16, kind=dbg_kind)
    lgT = nc.dram_tensor("lgT", [8, 96, T], BF16, kind=dbg_kind)
    mixr = nc.dram_tensor("mixr", [6, 128, T], BF16, kind=dbg_kind)
    XD = nc.dram_tensor("XD", [XD_ROWS, 2048], BF16)
    XO = nc.dram_tensor("XO", [XO_ROWS, 2048], BF16)
    RS = nc.dram_tensor("RS", [768, 128], F32)
    XDG = [nc.dram_tensor(f"XDG{l}", [NCORES * XD_ROWS, 2048], BF16) for l in range(L)]
    XOG = [nc.dram_tensor(f"XOG{l}", [NCORES * XO_ROWS, 2048], BF16) for l in range(L)]
    RSG = [nc.dram_tensor(f"RSG{l}", [NR * 768, 128], F32) for l in range(L)]
    wib = [[nc.dram_tensor(f"wib{l}_{i}", [D // NCORES, w_], BF16) for i, (a_, w_) in enumerate(WCH)] for l in range(L)]
    wob = [nc.dram_tensor(f"wob{l}", [D // NCORES, D], BF16) for l in range(L)]
    WI = [[nc.dram_tensor(f"WI{l}_{i}", [D, w_], BF16) for i, (a_, w_) in enumerate(WCH)] for l in range(L)]
    WO = [nc.dram_tensor(f"WO{l}", [D, D], BF16) for l in range(L)]
    XDL = nc.dram_tensor("XDL", [NR, 576, 2048], BF16, kind=dbg_kind)
    XOL = nc.dram_tensor("XOL", [NR, 320, 2048], BF16, kind=dbg_kind)
    dbgs = {}
    if debug:
        dbgs["XDd"] = nc.dram_tensor("XDd", [XD_ROWS, 2048], BF16, kind="ExternalOutput")
        dbgs["XOd"] = nc.dram_tensor("XOd", [XO_ROWS, 2048], BF16, kind="ExternalOutput")

    cc_sem = nc.alloc_semaphore("cc_sem")
    cc_cnt = [0]
    fin = Ring(kb, "fin", 1)

    with ExitStack() as gs:
        def gsb(name, shape, dt):
            return gs.enter_context(nc.sbuf_tensor(name, shape, dt))

        ident = gsb("ident", [128, 128], BF16)
        maskT = gsb("maskT", [128, 128], BF16)
        COS = gsb("COS", [128, NTT, 96], F32)
        SIN = gsb("SIN", [128, NTT, 96], F32)
        dec = gsb("dec", [128, 12], F32)
        gcb = gsb("gcb", [128, 6], F32)
        cfb = gsb("cfb", [128, 18], F32)
        ld = Ring(kb, "ld", 1)

        with ExitStack() as es:
            def sb(name, shape, dt):
                return es.enter_context(nc.sbuf_tensor(uniq(name), shape, dt))
            posi = sb("posi", [128, NTT], I32)
            posf = sb("posf", [128, NTT], F32)
            invb = sb("invb", [128, 96], F32)
            ANG = sb("ANG", [128, NTT * 96], F32)
            A2 = sb("A2", [128, NTT * 96], F32)
            KF = sb("KF", [128, NTT * 96], F32)
            KI = sb("KI", [128, NTT * 96], I32)
            ld.dma("sp", 0, ident[:, :], c_ident[:, :])
            ld.dma("sp", 0, maskT[:, :], c_mask[:, :])
            ld.dma("sp", 0, dec[:, :], c_dec[:, :])
            ld.dma("sp", 0, gcb[:, :], c_gc.ap().partition_broadcast(128))
            ld.dma("sp", 0, cfb[:, :], c_cf.ap().partition_broadcast(128))
            ld.dma("sp", 0, posi[:, :], pos_in[:, :])
            t_ld = ld.dma("sp", 0, invb[:, :], c_invf.ap().partition_broadcast(128))
            t = kb.op("dve", "tensor_copy", out=posf[:, :], in_=posi[:, :], w=[t_ld], m=True)
            for tt in range(NTT):
                t2 = kb.op("dve", "tensor_scalar", out=ANG[:, tt * 96:(tt + 1) * 96], in0=invb[:, :],
                           scalar1=posf[:, tt:tt + 1], scalar2=None, op0=ALU.mult, w=[t], m=(tt == NTT - 1))
            TWO_PI = 2.0 * math.pi
            C1 = 6.28125
            C2 = TWO_PI - C1

            def sincos(dst, shift, t_in):
                ta = kb.op("dve", "tensor_scalar", out=KF[:, :], in0=ANG[:, :], scalar1=float(1.0 / TWO_PI),
                           scalar2=float(shift / TWO_PI), op0=ALU.mult, op1=ALU.add, w=[t_in], m=True)
                tb = kb.op("dve", "tensor_copy", out=KI[:, :], in_=KF[:, :], w=[ta], m=True)
                tc = kb.op("dve", "tensor_copy", out=KF[:, :], in_=KI[:, :], w=[tb], m=True)
                td = kb.op("dve", "scalar_tensor_tensor", out=A2[:, :], in0=KF[:, :], scalar=float(-C1), in1=ANG[:, :],
                           op0=ALU.mult, op1=ALU.add, w=[tc], m=True)
                te = kb.op("dve", "scalar_tensor_tensor", out=A2[:, :], in0=KF[:, :], scalar=float(-C2), in1=A2[:, :],
                           op0=ALU.mult, op1=ALU.add, w=[td], m=True)
                tf = kb.op("dve", "tensor_scalar", out=A2[:, :], in0=A2[:, :], scalar1=float(shift), scalar2=None,
                           op0=ALU.add, w=[te], m=True)
                tg = kb.op("dve", "tensor_scalar", out=A2[:, :], in0=A2[:, :], scalar1=3.1415925, scalar2=-3.1415925,
                           op0=ALU.min, op1=ALU.max, w=[tf], m=True)
                th = kb.op("act", "activation", out=dst[:, :, :].rearrange("p a b -> p (a b)"), in_=A2[:, :],
                           func=AF.Sin, w=[tg], m=True)
                return th

            t_s = sincos(SIN, 0.0, t2)
            t_c = sincos(COS, math.pi / 2.0, t_s)
            kb.barrier([ld])

        rcast = Ring(kb, "rcast", 1)
        t_wi, t_wo = [], []
        cast_tok = {}
        for l in range(L):
            for i, (a_, w_) in enumerate(WCH):
                cast_tok[(l, i)] = rcast.dma("pool", 0, wib[l][i][:, :], w_in_s[l, :, a_:a_ + w_])
            cast_tok[(l, "o")] = rcast.dma("pool", 0, wob[l][:, :], w_out_s[l, :, :])
        ALL8 = [list(range(NCORES))]

        def allgather(src, dst, wait_toks, groups=GROUPS):
            for t_ in wait_toks:
                kb.wait("pool", t_)
            ins = nc.gpsimd.collective_compute(
                "AllGather", ALU.bypass, replica_groups=groups,
                ins=[src.ap().opt()], outs=[dst.ap().opt()])
            cc_cnt[0] += 1
            ins.then_inc(cc_sem, 1)
            tok = ("d", cc_sem, cc_cnt[0], ("cc",))
            kb.wait("pool", tok)
            return tok

        wg_tok = {}

        def ensure_w(l, i, block=False):
            if (l, i) in wg_tok:
                return wg_tok[(l, i)]
            src, dst = (wob[l], WO[l]) if i == "o" else (wib[l][i], WI[l][i])
            wg_tok[(l, i)] = allgather(src, dst, [cast_tok[(L - 1, "o")]], groups=ALL8)
            if block:
                kb.wait("pool", wg_tok[(l, i)])
            return wg_tok[(l, i)]
        rcp = Ring(kb, "rcp", 1)
        pid_p = nc.gpsimd.partition_id()

        def copy_local(src_g, dst_l, rows, t_ag):
            base = (pid_p // 4) * (4 * 4 * rows) + (pid_p % 4) * rows
            flat = src_g.ap()
            src = bass.AP(flat.tensor, base * 2048, [[4 * rows * 2048, NR], [2048, rows], [1, 2048]])
            return rcp.dma("pool", 0, dst_l.ap(), src, w=[t_ag])

        def phase1(l, x_src, ag_box):
            with ExitStack() as es:
                def sb(name, shape, dt):
                    return es.enter_context(nc.sbuf_tensor(uniq(name), shape, dt))

                def psb(name, shape, dt):
                    return es.enter_context(nc.psum_tensor(uniq(name), shape, dt))
                hT = sb("hT", [128, 16, T], BF16)
                wt = [sb(f"wt{i}", [128, 16, 512], BF16) for i in range(2)]
                xt = [sb(f"xt{i}", [128, D], F32) for i in range(2)]
                hb = sb("hb", [128, D], BF16)
                gbc = sb("gbc", [128, D], F32)
                junk = sb("junk", [128, D], BF16)
                ssq = sb("ssq", [128, NTT], F32)
                ms = sb("ms", [128, NTT], F32)
                lnv = sb("lnv", [128, NTT], F32)
                rstd = sb("rstd", [128, NTT], F32)
                X32 = [sb(f"X32_{i}", [128, 768], F32) for i in range(2)]
                tA = sb("tA", [128, 384], F32)
                tB = sb("tB", [128, 384], F32)
                tC = sb("tC", [128, 384], F32)
                tD = sb("tD", [128, 384], F32)
                O16 = [sb(f"O16_{i}", [128, 512], BF16) for i in range(3)]
                OT16 = [sb(f"OT16_{i}", [128, 512], BF16) for i in range(2)]
                F16 = [sb(f"F16_{i}", [96, 512], BF16) for i in range(2)]
                pproj = [psb(f"pproj{i}", [128, 512], F32) for i in range(3)]
                ptq = [psb(f"ptq{i}", [128, 1024], BF16) for i in range(2)]
                ptr = [psb(f"ptr{i}", [128, 1024], BF16) for i in range(2)]
                rx = Ring(kb, f"rx{l}_", 2)
                rw = Ring(kb, f"rw{l}_", 2)
                rO = Ring(kb, f"rO{l}_", 3)
                rOT = Ring(kb, f"rOT{l}_", 2)
                rF = Ring(kb, f"rF{l}_", 2)
                rg_ = Ring(kb, f"rgl{l}_", 1)

                t_g = rg_.dma("sp", 0, gbc[:, :], pre_g[l, :].partition_broadcast(128))
                ngr = len(COLG)

                def load_w(gi, waits):
                    kind, c0, w, h0 = COLG[gi]
                    ci, lc = wchunk(c0)
                    for g2 in (gi, gi + 1, gi + 2):
                        if g2 < len(COLG):
                            ensure_w(l, wchunk(COLG[g2][1])[0])
                    return rw.dma("sp", gi % 2, wt[gi % 2][:, :, 0:w],
                                  WI[l][ci][:, lc:lc + w].rearrange("(k p) c -> p k c", p=128),
                                  w=waits + [ensure_w(l, ci)])
                wtok = {0: load_w(0, [])}
                wt_last_mm = {}

                st = dict(u=0, xs=0, o=0, ot=0, f=0, tq=0)
                bank_free = [None, None, None]
                x32_free = [None, None]
                o_free = [[], [], []]
                ptq_free = [None, None]
                hT_tok = {}
                xt_free = [[], []]
                hb_free = [None]
                ptr_free = [None]

                def norm_tile(tt):
                    s = tt % 2
                    t_x = rx.dma("sp", s, xt[s][:, :], x_src[tt * 128:(tt + 1) * 128, :], w=xt_free[s])
                    t_sq = kb.op("act", "activation", out=junk[:, :], in_=xt[s][:, :], func=AF.Square,
                                 accum_out=ssq[:, tt:tt + 1], w=[t_x], m=True)
                    t_ms = kb.op("dve", "tensor_scalar", out=ms[:, tt:tt + 1], in0=ssq[:, tt:tt + 1], scalar1=1.0 / D,
                                 scalar2=EPS, op0=ALU.mult, op1=ALU.add, w=[t_sq], m=True)
                    t_ln = kb.op("act", "activation", out=lnv[:, tt:tt + 1], in_=ms[:, tt:tt + 1], func=AF.Ln,
                                 w=[t_ms], m=True)
                    t_rs = kb.op("act", "activation", out=rstd[:, tt:tt + 1], in_=lnv[:, tt:tt + 1], func=AF.Exp,
                                 scale=-0.5, w=[t_ln], m=True)
                    t_h = kb.op("dve", "scalar_tensor_tensor", out=hb[:, :], in0=xt[s][:, :],
                                scalar=rstd[:, tt:tt + 1], in1=gbc[:, :], op0=ALU.mult, op1=ALU.mult,
                                w=[t_rs, t_g, hb_free[0]], m=True)
                    xt_free[s] = [t_sq, t_h]
                    for k in range(16):
                        t_tr = kb.op("pe", "transpose", out=ptr[k // 8][:, (k % 8) * 128:(k % 8 + 1) * 128],
                                     in_=hb[:, k * 128:(k + 1) * 128], identity=ident[:, :],
                                     w=[t_h, ptr_free[0]], m=(k == 15))
                    hb_free[0] = t_tr
                    for hh in range(2):
                        t_ev = kb.op("act", "tensor_copy" if False else "activation",
                                     out=hT[:, hh * 8:(hh + 1) * 8, tt * 128:(tt + 1) * 128],
                                     in_=ptr[hh][:, :].rearrange("p (a b) -> p a b", a=8), func=AF.Copy,
                                     w=[t_tr], m=True)
                    ptr_free[0] = t_ev
                    hT_tok[tt] = t_ev

                def rope(xs, oslot, w, tt, nfreq, tab0, t_in):
                    nb = w // (2 * nfreq)
                    hw = w // 2
                    Xv = X32[xs][:, 0:w].rearrange("p (h two f) -> p h two f", two=2, f=nfreq)
                    Ov = O16[oslot][:, 0:w].rearrange("p (h two f) -> p h two f", two=2, f=nfreq)
                    X0, X1 = Xv[:, :, 0, :], Xv[:, :, 1, :]
                    Cb = COS[:, tt, tab0:tab0 + nfreq].unsqueeze(1).to_broadcast([128, nb, nfreq])
                    Sb = SIN[:, tt, tab0:tab0 + nfreq].unsqueeze(1).to_broadcast([128, nb, nfreq])

                    def tv(tl):
                        return tl[:, 0:hw].rearrange("p (h f) -> p h f", f=nfreq)
                    kb.op("dve", "tensor_tensor", out=tv(tA), in0=X1, in1=Sb, op=ALU.mult, w=[t_in] + o_free[oslot])
                    t1 = kb.op("dve", "tensor_tensor", out=tv(tB), in0=X0, in1=Cb, op=ALU.mult, m=True)
                    kb.op("dve", "tensor_tensor", out=tv(tC), in0=X0, in1=Sb, op=ALU.mult)
                    t2_ = kb.op("dve", "tensor_tensor", out=tv(tD), in0=X1, in1=Cb, op=ALU.mult, m=True)
                    kb.op("dve", "tensor_tensor", out=Ov[:, :, 0, :], in0=tv(tB), in1=tv(tA), op=ALU.subtract, w=[t1])
                    t3 = kb.op("dve", "tensor_tensor", out=Ov[:, :, 1, :], in0=tv(tD), in1=tv(tC), op=ALU.add,
                               w=[t2_], m=True)
                    x32_free[xs] = t2_
                    return t3

                def transposes_out(oslot, w, t_in, dst_ap_fn, tt):
                    nh = w // 128
                    q = st["tq"] % 2
                    st["tq"] += 1
                    for hh in range(nh):
                        t_tr = kb.op("pe", "transpose", out=ptq[q][:, hh * 128:(hh + 1) * 128],
                                     in_=O16[oslot][:, hh * 128:(hh + 1) * 128], identity=ident[:, :],
                                     w=[t_in, ptq_free[q]], m=(hh == nh - 1))
                    os_ = st["ot"] % 2
                    st["ot"] += 1
                    t_ot = kb.op("dve", "tensor_copy", out=OT16[os_][:, 0:w], in_=ptq[q][:, 0:w],
                                 w=[t_tr, rOT.tok(os_)], m=True)
                    ptq_free[q] = t_ot
                    rOT.dma("sp", os_, dst_ap_fn(nh, tt), OT16[os_][:, 0:w].rearrange("p (h t) -> p h t", t=128),
                            w=[t_ot])
                    return t_tr

                XD2 = XD.ap()

                pending = []

                def run_pending():
                    while pending:
                        pending.pop(0)()

                def unit_tokmajor(gi, tt):
                    kind, c0, w, h0 = COLG[gi]
                    u = st["u"]
                    st["u"] += 1
                    pb = pproj[u % 3]
                    waits = [wtok[gi], hT_tok[tt], bank_free[u % 3]]
                    for k in range(16):
                        t_mm = kb.op("pe", "matmul", pb[:, 0:w], lhsT=hT[:, k, tt * 128:(tt + 1) * 128],
                                     rhs=wt[gi % 2][:, k, 0:w], start=(k == 0), stop=(k == 15),
                                     w=(waits if k == 0 else []), m=(k == 15))
                    wt_last_mm[gi] = t_mm
                    run_pending()
                    nh = w // 128
                    osl = st["o"] % 3
                    st["o"] += 1
                    if kind in ("rq", "rk"):
                        xs = st["xs"] % 2
                        st["xs"] += 1
                        dcol = (0 if kind == "rq" else 6) + h0
                        for hh in range(nh):
                            t_e = kb.op("act", "activation", out=X32[xs][:, hh * 128:(hh + 1) * 128],
                                        in_=pb[:, hh * 128:(hh + 1) * 128], func=AF.Copy,
                                        scale=dec[:, dcol + hh:dcol + hh + 1], w=[t_mm, x32_free[xs]],
                                        m=(hh == nh - 1))
                        bank_free[u % 3] = t_e
                        t_r = rope(xs, osl, w, tt, 64, 0, t_e)
                        o_free[osl] = []
                        if kind == "rk":
                            o_free[osl].append(rO.dma("sp", osl, rk[tt * 128:(tt + 1) * 128, h0 * 128:h0 * 128 + w],
                                                      O16[osl][:, 0:w], w=[t_r]))
                        dstT = rqT if kind == "rq" else rkT

                        def _deferred(osl=osl, w=w, t_r=t_r, dstT=dstT, h0=h0, tt=tt):
                            t_tr = transposes_out(
                                osl, w, t_r,
                                lambda nh_, tt_: dstT[h0:h0 + nh_, :, tt_ * 128:(tt_ + 1) * 128].rearrange("h d t -> d h t"),
                                tt)
                            o_free[osl].append(t_tr)
                        pending.append(_deferred)
                    elif kind in ("dq", "dk"):
                        xs = st["xs"] % 2
                        st["xs"] += 1
                        t_e = kb.op("act", "activation", out=X32[xs][:, 0:w], in_=pb[:, 0:w], func=AF.Copy,
                                    w=[t_mm, x32_free[xs]], m=True)
                        bank_free[u % 3] = t_e
                        t_r = rope(xs, osl, w, tt, 32, 64, t_e)
                        r0 = 0 if kind == "dq" else 128
                        o_free[osl] = []

                        def _deferred(osl=osl, w=w, t_r=t_r, r0=r0, tt=tt):
                            t_tr = transposes_out(
                                osl, w, t_r,
                                lambda nh_, tt_: XD2.rearrange("(h r) c -> r h c", h=4)[
                                    r0:r0 + 128, :, tt_ * 128:(tt_ + 1) * 128],
                                tt)
                            o_free[osl].append(t_tr)
                        pending.append(_deferred)
                    else:
                        func = AF.Silu if kind in ("rg", "dg") else AF.Copy
                        t_e = kb.op("act", "activation", out=O16[osl][:, 0:w], in_=pb[:, 0:w], func=func,
                                    w=[t_mm] + o_free[osl], m=True)
                        bank_free[u % 3] = t_e
                        if kind == "dv":
                            dst = XD2.rearrange("(h r) c -> h r c", h=4)[
                                :, 256 + tt * 8:256 + tt * 8 + 8, :].rearrange("h r (a e) -> (r a) h e", e=128)
                            src = O16[osl][:, 0:w].rearrange("p (h e) -> p h e", e=128)
                        else:
                            dd = {"rv": rv, "rg": rg, "dg": dg}[kind]
                            dst = dd[tt * 128:(tt + 1) * 128, h0 * 128:h0 * 128 + w]
                            src = O16[osl][:, 0:w]
                        o_free[osl] = [rO.dma("sp", osl, dst, src, w=[t_e])]

                def unit_featmajor(gi, n, tg):
                    kind, c0, w, h0 = COLG[gi]
                    u = st["u"]
                    st["u"] += 1
                    pb = pproj[u % 3]
                    waits = [wtok[gi], bank_free[u % 3]]
                    for k in range(16):
                        t_mm = kb.op("pe", "matmul", pb[0:96, 0:512], lhsT=wt[gi % 2][:, k, n * 96:(n + 1) * 96],
                                     rhs=hT[:, k, tg * 512:(tg + 1) * 512], start=(k == 0), stop=(k == 15),
                                     w=(waits if k == 0 else []), m=(k == 15))
                    wt_last_mm[gi] = t_mm
                    run_pending()
                    fs = st["f"] % 2
                    st["f"] += 1
                    func = AF.Silu if kind == "lg" else AF.Copy
                    t_e = kb.op("act", "activation", out=F16[fs][:, :], in_=pb[0:96, 0:512], func=func,
                                w=[t_mm, rF.tok(fs)], m=True)
                    bank_free[u % 3] = t_e
                    nbk = h0 + n
                    if kind == "lx":
                        xr = (nbk // 2) * 576 + 384 + (nbk % 2) * 96
                        dst = XD2[xr:xr + 96, tg * 512:(tg + 1) * 512]
                    else:
                        dst = lgT[nbk, :, tg * 512:(tg + 1) * 512]
                    rF.dma("sp", fs, dst, F16[fs][:, :], w=[t_e])

                for gi in range(ngr):
                    kind = COLG[gi][0]
                    if gi == N_XD_GROUPS:
                        run_pending()
                        for i_ in range(len(WCH)):
                            ensure_w(l, i_)
                        ag_box.append(allgather(XD, XDG[l], rO.all_toks() + rOT.all_toks() + rF.all_toks(), groups=ALL8))
                        ensure_w(l, "o")
                    if gi + 1 < ngr:
                        prev = [wt_last_mm[gi - 1]] if gi >= 1 else []
                        wtok[gi + 1] = load_w(gi + 1, prev)
                    if kind in ("lx", "lg"):
                        for n in range(4):
                            for tg in range(4):
                                unit_featmajor(gi, n, tg)
                    else:
                        for tt in range(NTT):
                            if gi == 0:
                                norm_tile(tt)
                            unit_tokmajor(gi, tt)
                run_pending()
                kb.barrier([rx, rw, rO, rOT, rF, rg_])

        def phaseR(l, mid_hook):
            with ExitStack() as es:
                def sb(name, shape, dt):
                    return es.enter_context(nc.sbuf_tensor(uniq(name), shape, dt))

                def psb(name, shape, dt):
                    return es.enter_context(nc.psum_tensor(uniq(name), shape, dt))
                kt = [sb(f"kt{i}", [128, 768], BF16) for i in range(2)]
                vt = [sb(f"vt{i}", [128, 768], BF16) for i in range(2)]
                kTt = [sb(f"kTt{i}", [128, 6, 128], BF16) for i in range(2)]
                qTt = [sb(f"qTt{i}", [128, 6, 128], BF16) for i in range(2)]
                gt = [sb(f"gt{i}", [128, 768], BF16) for i in range(2)]
                Sst = sb("Sst", [128, 6, 128], F32)
                Stmp = sb("Stmp", [128, 6, 128], F32)
                Sb16 = sb("Sb16", [128, 6, 128], BF16)
                SG = sb("SG", [128, 4, 768], F32)
                pT = sb("pT", [128, 6, 128], BF16)
                O32 = [sb(f"O32_{i}", [128, 768], F32) for i in range(2)]
                SQ = sb("SQ", [128, 768], F32)
                RN = sb("RN", [128, 768], F32)
                RO = sb("RO", [128, 768], BF16)
                ROT = [sb(f"ROT{i}", [128, 768], BF16) for i in range(2)]
                stt = sb("stt", [128, 6], F32)
                stm = sb("stm", [128, 6], F32)
                stl = sb("stl", [128, 6], F32)
                str_ = sb("str", [128, 6], F32)
                ps_a = [psb(f"psa{i}", [128, 512], F32) for i in range(2)]
                ps_o = [psb(f"pso{i}", [128, 512], F32) for i in range(2)]
                ps_k = [psb(f"psk{i}", [128, 512], F32) for i in range(2)]
                ps_t = psb("pst", [128, 1024], BF16)
                r1 = Ring(kb, f"r1_{l}_", 2)
                r2 = Ring(kb, f"r2_{l}_", 2)
                rs = Ring(kb, f"rrs{l}_", 1)
                rot = Ring(kb, f"rot{l}_", 2)

                def banks(ps, h):
                    return ps[h // 4][:, (h % 4) * 128:(h % 4 + 1) * 128]
                gcB = gcb[:, :].unsqueeze(2).to_broadcast([128, 6, 128])

                def state_update(ps, t_mm, extra_w):
                    kb.op("dve", "tensor_tensor", out=Stmp[:, 0:4, :].rearrange("p h e -> p (h e)"),
                          in0=ps[0][:, :], in1=Sst[:, 0:4, :].rearrange("p h e -> p (h e)"), op=ALU.add,
                          w=[t_mm] + extra_w)
                    ta = kb.op("dve", "tensor_tensor", out=Stmp[:, 4:6, :].rearrange("p h e -> p (h e)"),
                               in0=ps[1][:, 0:256], in1=Sst[:, 4:6, :].rearrange("p h e -> p (h e)"), op=ALU.add, m=True)
                    tb = kb.op("dve", "tensor_tensor", out=Sst[:, :, :], in0=Stmp[:, :, :], in1=gcB, op=ALU.mult,
                               w=[ta], m=True)
                    return ta, tb

                t_z = kb.op("dve", "memset", Sst[:, :, :].rearrange("p h e -> p (h e)"), 0.0, m=True)
                free1 = [[], []]
                t_S = t_z
                kv_free = None
                for n in range(NTT):
                    s = n % 2
                    r1.dma("sp", s, kt[s][:, :], rk[n * 128:(n + 1) * 128, :], w=free1[s])
                    t_l = r1.dma("sp", s, vt[s][:, :], rv[n * 128:(n + 1) * 128, :])
                    for h in range(6):
                        t_mm = kb.op("pe", "matmul", banks(ps_k, h), lhsT=kt[s][:, h * 128:(h + 1) * 128],
                                     rhs=vt[s][:, h * 128:(h + 1) * 128], start=True, stop=True,
                                     w=[t_l, kv_free], m=(h == 5))
                    free1[s] = [t_mm]
                    ta, t_S = state_update(ps_k, t_mm, [t_S])
                    kv_free = ta
                t_rs = rs.dma("sp", 0, RS.ap().rearrange("(h d) e -> d h e", h=6), Sst[:, :, :], w=[t_S])
                t_cc = allgather(RS, RSG[l], [t_rs])
                mid_hook()
                t_g = rs.dma("sp", 0, SG[:, :, :].rearrange("p r (h e) -> p r h e", h=6),
                             RSG[l].ap().rearrange("(r h d) e -> d r h e", r=4, h=6), w=[t_cc, t_rs])

                def cfB(i):
                    return cfb[:, i * 6:(i + 1) * 6].unsqueeze(2).to_broadcast([128, 6, 128])

                def SGv(i):
                    return SG[:, i, :].rearrange("p (h e) -> p h e", h=6)
                t0 = kb.op("dve", "tensor_tensor", out=Sst[:, :, :], in0=SGv(0), in1=cfB(0), op=ALU.mult,
                           w=[t_g, t_S], m=True)
                for i in (1, 2):
                    t1 = kb.op("dve", "tensor_tensor", out=Stmp[:, :, :], in0=SGv(i), in1=cfB(i), op=ALU.mult,
                               w=[t0], m=True)
                    t0 = kb.op("dve", "tensor_tensor", out=Sst[:, :, :], in0=Sst[:, :, :], in1=Stmp[:, :, :],
                               op=ALU.add, w=[t1], m=True)
                t_S = t0
                kb.barrier([r1, rs])

                free2 = [[], []]
                maskB = maskT[:, :].unsqueeze(1).to_broadcast([128, 4, 128])
                maskB2 = maskT[:, :].unsqueeze(1).to_broadcast([128, 2, 128])
                stx = dict(sc_free=None, o_free=None, pT_free=None, sb_free=None, kv_free=kv_free, t_S=t_S,
                           pst_free=None, ro_free=None)
                o32_free = [None, None]
                o32_tok = {}

                def stageX(n):
                    s = n % 2
                    c0 = n * 128
                    r2.dma("sp", s, kt[s][:, :], rk[c0:c0 + 128, :], w=free2[s])
                    r2.dma("sp", s, vt[s][:, :], rv[c0:c0 + 128, :])
                    r2.dma("sp", s, gt[s][:, :], rg[c0:c0 + 128, :])
                    r2.dma("sp", s, kTt[s][:, :, :], rkT[:, :, c0:c0 + 128].rearrange("h d t -> d h t"))
                    t_l = r2.dma("sp", s, qTt[s][:, :, :], rqT[:, :, c0:c0 + 128].rearrange("h d t -> d h t"))
                    for h in range(6):
                        t_sc = kb.op("pe", "matmul", banks(ps_a, h), lhsT=kTt[s][:, h, :], rhs=qTt[s][:, h, :],
                                     start=True, stop=True, w=[t_l, stx["sc_free"]], m=(h == 5))
                    kb.op("dve", "tensor_tensor", out=pT[:, 0:4, :], in0=ps_a[0][:, :].rearrange("p (h i) -> p h i", h=4),
                          in1=maskB, op=ALU.mult, w=[t_sc, stx["pT_free"]])
                    t_p = kb.op("dve", "tensor_tensor", out=pT[:, 4:6, :],
                                in0=ps_a[1][:, 0:256].rearrange("p (h i) -> p h i", h=2), in1=maskB2, op=ALU.mult, m=True)
                    stx["sc_free"] = t_p
                    t_sb = kb.op("act", "activation", out=Sb16[:, :, :].rearrange("p h e -> p (h e)"),
                                 in_=Sst[:, :, :].rearrange("p h e -> p (h e)"), func=AF.Copy,
                                 w=[stx["t_S"], stx["sb_free"]], m=True)
                    for h in range(6):
                        kb.op("pe", "matmul", banks(ps_o, h), lhsT=pT[:, h, :], rhs=vt[s][:, h * 128:(h + 1) * 128],
                              start=True, stop=False, w=[t_p, stx["o_free"]])
                        t_o = kb.op("pe", "matmul", banks(ps_o, h), lhsT=qTt[s][:, h, :], rhs=Sb16[:, h, :],
                                    start=False, stop=True, w=[t_sb], m=(h == 5))
                    stx["pT_free"] = t_o
                    stx["sb_free"] = t_o
                    for h in range(6):
                        t_kv = kb.op("pe", "matmul", banks(ps_k, h), lhsT=kt[s][:, h * 128:(h + 1) * 128],
                                     rhs=vt[s][:, h * 128:(h + 1) * 128], start=True, stop=True,
                                     w=[stx["kv_free"]], m=(h == 5))
                    free2[s] = [t_kv, t_o]
                    ta, tS = state_update(ps_k, t_kv, [stx["t_S"], t_sb])
                    stx["kv_free"] = ta
                    stx["t_S"] = tS
                    kb.op("act", "activation", out=O32[s][:, 0:512], in_=ps_o[0][:, :], func=AF.Copy,
                          w=[t_o, o32_free[s]])
                    t_e = kb.op("act", "activation", out=O32[s][:, 512:768], in_=ps_o[1][:, 0:256], func=AF.Copy, m=True)
                    stx["o_free"] = t_e
                    o32_tok[n] = t_e

                def stageY(n):
                    s = n % 2
                    c0 = n * 128
                    t_e = o32_tok[n]
                    t_q = kb.op("dve", "tensor_tensor", out=SQ[:, :], in0=O32[s][:, :], in1=O32[s][:, :], op=ALU.mult,
                                w=[t_e], m=True)
                    t_r = kb.op("dve", "tensor_reduce", out=stt[:, :], in_=SQ[:, :].rearrange("p (h e) -> p h e", h=6),
                                axis=AX.X, op=ALU.add, w=[t_q], m=True)
                    t_m = kb.op("dve", "tensor_scalar", out=stm[:, :], in0=stt[:, :], scalar1=1.0 / 128, scalar2=EPS,
                                op0=ALU.mult, op1=ALU.add, w=[t_r], m=True)
                    t_ln = kb.op("act", "activation", out=stl[:, :], in_=stm[:, :], func=AF.Ln, w=[t_m], m=True)
                    t_rs_ = kb.op("act", "activation", out=str_[:, :], in_=stl[:, :], func=AF.Exp, scale=-0.5,
                                  w=[t_ln], m=True)
                    t_n = kb.op("dve", "tensor_tensor", out=RN[:, :].rearrange("p (h e) -> p h e", h=6),
                                in0=O32[s][:, :].rearrange("p (h e) -> p h e", h=6),
                                in1=str_[:, :].unsqueeze(2).to_broadcast([128, 6, 128]), op=ALU.mult, w=[t_rs_], m=True)
                    o32_free[s] = t_n
                    t_ro = kb.op("dve", "tensor_tensor", out=RO[:, :], in0=RN[:, :], in1=gt[s][:, :], op=ALU.mult,
                                 w=[t_n, stx["ro_free"]], m=True)
                    free2[s].append(t_ro)
                    for h in range(6):
                        t_tr = kb.op("pe", "transpose", out=ps_t[:, h * 128:(h + 1) * 128],
                                     in_=RO[:, h * 128:(h + 1) * 128], identity=ident[:, :],
                                     w=[t_ro, stx["pst_free"]], m=(h == 5))
                    stx["ro_free"] = t_tr
                    t_ev = kb.op("act", "activation", out=ROT[s][:, :], in_=ps_t[:, 0:768], func=AF.Copy,
                                 w=[t_tr, rot.tok(s)], m=True)
                    stx["pst_free"] = t_ev
                    rot.dma("sp", s, mixr[:, :, c0:c0 + 128].rearrange("h e t -> e h t"),
                            ROT[s][:, :].rearrange("p (h t) -> p h t", h=6), w=[t_ev])

                stageX(0)
                for n in range(NTT):
                    if n + 1 < NTT:
                        stageX(n + 1)
                    stageY(n)
                kb.barrier([r2, rot])

        def phase2(l, t_ag):
            lam_init = 0.8 - 0.6 * math.exp(-0.3 * l)
            with ExitStack() as es:
                def sb(name, shape, dt):
                    return es.enter_context(nc.sbuf_tensor(uniq(name), shape, dt))

                def psb(name, shape, dt):
                    return es.enter_context(nc.psum_tensor(uniq(name), shape, dt))
                XO2 = XO.ap()
                rl = Ring(kb, f"p2l{l}_", 1)
                rst = Ring(kb, f"p2s{l}_", 2)
                lp = sb("lp", [96, 2, 8], F32)
                wab = sb("wab", [96, 2, 2, 96], BF16)
                dl = sb("dl", [128, 4, 64], F32)
                gsb_ = sb("gsb", [128, 128], F32)
                lamt = sb("lamt", [128, 8], F32)
                c1 = sb("c1", [96, 2, 4], F32)
                rl.dma("sp", 0, lp[:, :, :], lru_p[l, :, :, :].rearrange("b c k -> c b k"))
                rwab = Ring(kb, f"rwab{l}_", 1)
                t_wab = rwab.dma("pool", 0, wab[:, :, :, :], lru_w[l, :, :, :, :].rearrange("b g k j -> k b g j"))
                rl.dma("sp", 0, dl[:, :, :].rearrange("p a b -> p (a b)"),
                       dlam[l, :, :].rearrange("a b -> (a b)").partition_broadcast(128))
                t_p = rl.dma("sp", 0, gsb_[:, :], subg[l, :].partition_broadcast(128))
                kb.op("dve", "tensor_tensor", out=dl[:, 0, :], in0=dl[:, 0, :], in1=dl[:, 1, :], op=ALU.mult, w=[t_p])
                t_a = kb.op("dve", "tensor_tensor", out=dl[:, 2, :], in0=dl[:, 2, :], in1=dl[:, 3, :], op=ALU.mult, m=True)
                kb.op("dve", "tensor_reduce", out=lamt[:, 0:1], in_=dl[:, 0, :], axis=AX.X, op=ALU.add, w=[t_a])
                t_b = kb.op("dve", "tensor_reduce", out=lamt[:, 1:2], in_=dl[:, 2, :], axis=AX.X, op=ALU.add, m=True)
                t_c = kb.op("act", "activation", out=lamt[:, 2:4], in_=lamt[:, 0:2], func=AF.Exp, w=[t_b], m=True)
                t_d = kb.op("dve", "tensor_tensor", out=lamt[:, 4:5], in0=lamt[:, 3:4], in1=lamt[:, 2:3], op=ALU.subtract,
                            w=[t_c], m=True)
                t_nl = kb.op("dve", "tensor_scalar", out=lamt[:, 5:6], in0=lamt[:, 4:5], scalar1=-lam_init, scalar2=None,
                             op0=ALU.add, w=[t_d], m=True)
                t_gs = kb.op("dve", "tensor_scalar", out=gsb_[:, :], in0=gsb_[:, :], scalar1=float(1.0 - lam_init),
                             scalar2=None, op0=ALU.mult, w=[t_p], m=True)
                t_e1 = kb.op("act", "activation", out=c1[:, :, 0:1], in_=lp[:, :, 7:8], func=AF.Exp, scale=-1.0,
                             w=[t_p], m=True)
                t_e2 = kb.op("act", "activation", out=c1[:, :, 1:2], in_=c1[:, :, 0:1], func=AF.Ln, bias=1.0,
                             w=[t_e1], m=True)
                kb.op("dve", "tensor_scalar", out=c1[:, :, 2:3], in0=c1[:, :, 1:2], scalar1=-8.0, scalar2=None,
                      op0=ALU.mult, w=[t_e2])
                t_c1 = kb.op("dve", "tensor_scalar", out=c1[:, :, 3:4], in0=c1[:, :, 1:2], scalar1=-16.0, scalar2=None,
                             op0=ALU.mult, m=True)

                CH = 2048
                xp = [sb(f"xp{i}", [96, CH + 3], BF16) for i in range(2)]
                xc = sb("xc", [96, CH], F32)
                xcb = sb("xcb", [96, CH], BF16)
                rr = sb("rr", [96, CH], F32)
                ii = sb("ii", [96, CH], F32)
                aa = sb("aa", [96, CH], F32)
                a2 = sb("a2", [96, CH], F32)
                bb = sb("bb", [96, CH], F32)
                hh_ = sb("hh", [96, CH], F32)
                hb16 = [sb(f"hb16_{i}", [96, CH], BF16) for i in range(2)]
                hprev = sb("hprev", [96, 2], F32)
                pg = [psb(f"pg{i}", [128, 512], F32) for i in range(4)]
                rxp = Ring(kb, f"rxp{l}_", 2)
                t_hp = kb.op("dve", "memset", hprev[:, :], 0.0, m=True)
                pg_free = [None] * 4
                xp_free = [[], []]
                cnt = 0
                t_h_prev = {0: t_hp, 1: t_hp}
                t_scan_prev = None
                hb_free = [None, None]
                t_misc_free = []
                for b in range(2):
                    for s in range(4):
                        xs = cnt % 2
                        cnt += 1
                        t_x = rxp.dma("sp", xs, xp[xs][:, 3:3 + CH], XDL[s, 384 + b * 96:384 + (b + 1) * 96, :],
                                      w=[t_ag] + xp_free[xs])
                        if s == 0:
                            t_halo = kb.op("dve", "memset", xp[xs][:, 0:3], 0.0, w=xp_free[xs], m=True)
                        else:
                            t_halo = kb.op("dve", "tensor_copy", out=xp[xs][:, 0:3], in_=xp[1 - xs][:, CH:CH + 3],
                                           w=xp_free[xs] + [t_xprev], m=True)
                        t_xprev = t_x
                        t_cv = kb.op("dve", "tensor_scalar", out=xc[:, :], in0=xp[xs][:, 0:CH], scalar1=lp[:, b, 0:1],
                                     scalar2=lp[:, b, 4:5], op0=ALU.mult, op1=ALU.add,
                                     w=[t_x, t_halo, t_p] + t_misc_free, m=True)
                        for j in (1, 2, 3):
                            t_cv = kb.op("dve", "scalar_tensor_tensor", out=xc[:, :], in0=xp[xs][:, j:j + CH],
                                         scalar=lp[:, b, j:j + 1], in1=xc[:, :], op0=ALU.mult, op1=ALU.add,
                                         w=[t_cv], m=True)
                        xp_free[xs] = [t_cv]
                        t_cb = kb.op("dve", "tensor_copy", out=xcb[:, :], in_=xc[:, :], w=[t_cv], m=True)
                        toks = []
                        for tg in range(4):
                            cs = slice(tg * 512, (tg + 1) * 512)
                            t_m1 = kb.op("pe", "matmul", pg[0 + (tg % 2) * 2][0:96, :], lhsT=wab[:, b, 0, :], rhs=xcb[:, cs],
                                         start=True, stop=True, w=[t_cb, rl.tok(0), t_wab, pg_free[(tg % 2) * 2]], m=True)
                            t_m2 = kb.op("pe", "matmul", pg[1 + (tg % 2) * 2][0:96, :], lhsT=wab[:, b, 1, :], rhs=xcb[:, cs],
                                         start=True, stop=True, w=[pg_free[1 + (tg % 2) * 2]], m=True)
                            t_r = kb.op("act", "activation", out=rr[:, cs], in_=pg[0 + (tg % 2) * 2][0:96, :],
                                        func=AF.Sigmoid, bias=lp[:, b, 5:6], w=[t_m1], m=True)
                            pg_free[(tg % 2) * 2] = t_r
                            t_i = kb.op("act", "activation", out=ii[:, cs], in_=pg[1 + (tg % 2) * 2][0:96, :],
                                        func=AF.Sigmoid, bias=lp[:, b, 6:7], w=[t_m2], m=True)
                            pg_free[1 + (tg % 2) * 2] = t_i
                            toks.append(t_i)
                        t_a1 = kb.op("act", "activation", out=aa[:, :], in_=rr[:, :], func=AF.Exp, scale=c1[:, b, 2:3],
                                     w=[t_c1, toks[-1]], m=True)
                        t_a2 = kb.op("act", "activation", out=a2[:, :], in_=rr[:, :], func=AF.Exp, scale=c1[:, b, 3:4],
                                     m=True)
                        t_om = kb.op("dve", "tensor_scalar", out=a2[:, :], in0=a2[:, :], scalar1=-1.0, scalar2=1.0,
                                     op0=ALU.mult, op1=ALU.add, w=[t_a2], m=True)
                        t_l1 = kb.op("act", "activation", out=a2[:, :], in_=a2[:, :], func=AF.Ln, w=[t_om], m=True)
                        t_l2 = kb.op("act", "activation", out=a2[:, :], in_=a2[:, :], func=AF.Exp, scale=0.5,
                                     w=[t_l1], m=True)
                        t_b1 = kb.op("dve", "tensor_tensor", out=bb[:, :], in0=ii[:, :], in1=xc[:, :], op=ALU.mult,
                                     w=[toks[-1], t_scan_prev], m=True)
                        t_b2 = kb.op("dve", "tensor_tensor", out=bb[:, :], in0=bb[:, :], in1=a2[:, :], op=ALU.mult,
                                     w=[t_b1, t_l2], m=True)
                        t_sc = kb.op("dve", "tensor_tensor_scan", out=hh_[:, :], data0=aa[:, :], data1=bb[:, :],
                                     initial=hprev[:, b:b + 1], op0=ALU.mult, op1=ALU.add,
                                     w=[t_b2, t_a1, t_h_prev[b]], m=True)
                        t_scan_prev = t_sc
                        t_h_prev[b] = kb.op("dve", "tensor_copy", out=hprev[:, b:b + 1], in_=hh_[:, CH - 1:CH],
                                            w=[t_sc], m=True)
                        hs = (cnt - 1) % 2
                        t_hb = kb.op("act", "activation", out=hb16[hs][:, :], in_=hh_[:, :], func=AF.Copy,
                                     w=[t_sc, rst.tok(hs)], m=True)
                        t_misc_free = [t_hb, t_b2, t_l2]
                        r0 = s * 320 + 128 + b * 96
                        rst.dma("sp", hs, XO2[r0:r0 + 96, :], hb16[hs][:, :], w=[t_hb])
                kb.barrier([rxp, rst, rl, rwab])

            with ExitStack() as es:
                def sb(name, shape, dt):
                    return es.enter_context(nc.sbuf_tensor(uniq(name), shape, dt))

                def psb(name, shape, dt):
                    return es.enter_context(nc.psum_tensor(uniq(name), shape, dt))
                XO2 = XO.ap()
                qT0 = sb("qT0", [128, S], BF16)
                qT1 = sb("qT1", [128, S], BF16)
                kT = sb("kT", [128, S], BF16)
                v1 = sb("v1", [128, 64, 132], BF16)
                Pm = [sb(f"Pm{i}", [128, 2, 512], BF16) for i in range(2)]
                accS = sb("accS", [128, 8, 132], F32)
                o1 = sb("o1", [128, 128], F32)
                oo = sb("oo", [128, 128], F32)
                sqj = sb("sqj", [128, 128], F32)
                on16 = [sb(f"on16_{i}", [128, 128], BF16) for i in range(2)]
                sm = sb("sm", [128, 16], F32)
                gsb2 = sb("gsb2", [128, 128], F32)
                lam2 = sb("lam2", [128, 2], F32)
                Sps = [[psb(f"Sps{i}_{c}", [128, 512], F32) for c in range(2)] for i in range(2)]
                accp = [psb(f"accp{i}", [128, 512], F32) for i in range(3)]
                ra = Ring(kb, f"ra{l}_", 1)
                ro = Ring(kb, f"ro{l}_", 2)
                lam_init = 0.8 - 0.6 * math.exp(-0.3 * l)
                dl = sb("dl2", [128, 4, 64], F32)
                lamt = sb("lamt2", [128, 8], F32)
                ra.dma("sp", 0, dl[:, :, :].rearrange("p a b -> p (a b)"),
                       dlam[l, :, :].rearrange("a b -> (a b)").partition_broadcast(128))
                t_p = ra.dma("sp", 0, gsb2[:, :], subg[l, :].partition_broadcast(128))
                kb.op("dve", "tensor_tensor", out=dl[:, 0, :], in0=dl[:, 0, :], in1=dl[:, 1, :], op=ALU.mult, w=[t_p])
                t_a = kb.op("dve", "tensor_tensor", out=dl[:, 2, :], in0=dl[:, 2, :], in1=dl[:, 3, :], op=ALU.mult, m=True)
                kb.op("dve", "tensor_reduce", out=lamt[:, 0:1], in_=dl[:, 0, :], axis=AX.X, op=ALU.add, w=[t_a])
                t_b = kb.op("dve", "tensor_reduce", out=lamt[:, 1:2], in_=dl[:, 2, :], axis=AX.X, op=ALU.add, m=True)
                t_c = kb.op("act", "activation", out=lamt[:, 2:4], in_=lamt[:, 0:2], func=AF.Exp, w=[t_b], m=True)
                t_d = kb.op("dve", "tensor_tensor", out=lamt[:, 4:5], in0=lamt[:, 3:4], in1=lamt[:, 2:3], op=ALU.subtract,
                            w=[t_c], m=True)
                t_nl = kb.op("dve", "tensor_scalar", out=lam2[:, 0:1], in0=lamt[:, 4:5], scalar1=-lam_init, scalar2=None,
                             op0=ALU.add, w=[t_d], m=True)
                t_gs = kb.op("dve", "tensor_scalar", out=gsb2[:, :], in0=gsb2[:, :], scalar1=float(1.0 - lam_init),
                             scalar2=None, op0=ALU.mult, w=[t_p], m=True)
                kb.op("pool", "memset", qT0[64:128, :], 0.0)
                kb.op("pool", "memset", qT1[0:64, :], 0.0)
                t_ms = kb.op("pool", "memset", v1[:, :, 128:132], 1.0, m=True)
                if l + 1 < nlayers:
                    for i_ in range(len(WCH)):
                        ensure_w(l + 1, i_, block=True)
                    ensure_w(l + 1, "o", block=True)
                for s in range(4):
                    cs = slice(s * 2048, (s + 1) * 2048)
                    ra.dma("sp", 0, qT0[0:64, cs], XDL[s, 0:64, :], w=[t_ag])
                    ra.dma("sp", 0, qT1[64:128, cs], XDL[s, 64:128, :])
                    ra.dma("sp", 0, kT[:, cs], XDL[s, 128:256, :])
                    t_ld = ra.dma("sp", 0, v1[:, s * 16:(s + 1) * 16, 0:128],
                                  XDL[s, 256:384, :].rearrange("r (a e) -> (r a) e", e=128).rearrange(
                                      "(a p) e -> p a e", p=128))
                pairs = []
                for G in range(16):
                    for j in range(4 * G + 4):
                        pairs.append((G, j))
                NP = len(pairs)
                s_free = [None, None]
                p_free = [None, None]
                t_qk = [None] * NP
                acc_free = None
                on_cnt = 0

                def slot_ap(c, q4, lo, hi):
                    idx = c * 4 + q4
                    return accp[idx // 3][:, (idx % 3) * 160 + lo:(idx % 3) * 160 + hi]

                def emit_qk(u):
                    G, j = pairs[u]
                    bf = u % 2
                    a = max(0, j - 4 * G)
                    c0 = a * 128
                    for c in range(2):
                        qs = qT0 if c == 0 else qT1
                        t_ = kb.op("pe", "matmul", Sps[bf][c][:, c0:512], lhsT=kT[:, j * 128:(j + 1) * 128],
                                   rhs=qs[:, G * 512 + c0:(G + 1) * 512], start=True, stop=True,
                                   w=[t_ld, t_ms, s_free[bf]], m=(c == 1))
                    t_qk[u] = t_

                emit_qk(0)
                for u in range(NP):
                    G, j = pairs[u]
                    bf = u % 2
                    a = max(0, j - 4 * G)
                    diag = j >= 4 * G
                    c0 = a * 128
                    if u + 1 < NP:
                        emit_qk(u + 1)
                    for c in range(2):
                        t_ex = kb.op("act", "activation", out=Pm[bf][:, c, c0:512], in_=Sps[bf][c][:, c0:512],
                                     func=AF.Exp, scale=0.125, w=[t_qk[u], p_free[bf]], m=(c == 1))
                    s_free[bf] = t_ex
                    t_pv_in = t_ex
                    if diag:
                        t_pv_in = kb.op("dve", "tensor_tensor", out=Pm[bf][:, :, c0:c0 + 128],
                                        in0=Pm[bf][:, :, c0:c0 + 128],
                                        in1=maskT[:, :].unsqueeze(1).to_broadcast([128, 2, 128]), op=ALU.mult,
                                        w=[t_ex], m=True)
                    first = (j == 0)
                    for c in range(2):
                        for q4 in range(a, 4):
                            idx = c * 4 + q4
                            st_flag = first and (idx % 3 == 0)
                            last = (c == 1 and q4 == 3)
                            t_pv = kb.op("pe", "matmul", slot_ap(c, q4, 0, 129), lhsT=Pm[bf][:, c, q4 * 128:(q4 + 1) * 128],
                                         rhs=v1[:, j, 0:129], start=st_flag, stop=(j == 4 * G + q4),
                                         skip_group_check=True,
                                         w=([t_pv_in, acc_free] if (c == 0 and q4 == a) else []), m=last)
                    p_free[bf] = t_pv
                    if j == 4 * G + 3:
                        for idx in range(8):
                            c, q4 = idx // 4, idx % 4
                            t_ev = kb.op("act", "activation", out=accS[:, idx, 0:129], in_=slot_ap(c, q4, 0, 129),
                                         func=AF.Copy, w=[t_pv], m=(idx == 7))
                        acc_free = t_ev
                        t_prev = t_ev
                        for q4 in range(4):
                            t1 = kb.op("dve", "reciprocal", out=sm[:, 0:1], in_=accS[:, q4, 128:129], w=[t_prev], m=True)
                            t2 = kb.op("dve", "reciprocal", out=sm[:, 1:2], in_=accS[:, 4 + q4, 128:129], m=True)
                            t3 = kb.op("dve", "tensor_tensor", out=sm[:, 2:3], in0=sm[:, 1:2], in1=lam2[:, 0:1],
                                       op=ALU.mult, w=[t2, t_nl], m=True)
                            t4 = kb.op("dve", "tensor_scalar", out=o1[:, :], in0=accS[:, q4, 0:128], scalar1=sm[:, 0:1],
                                       scalar2=None, op0=ALU.mult, w=[t1], m=True)
                            t5 = kb.op("dve", "scalar_tensor_tensor", out=oo[:, :], in0=accS[:, 4 + q4, 0:128],
                                       scalar=sm[:, 2:3], in1=o1[:, :], op0=ALU.mult, op1=ALU.add, w=[t3, t4], m=True)
                            t6 = kb.op("act", "activation", out=sqj[:, :], in_=oo[:, :], func=AF.Square,
                                       accum_out=sm[:, 3:4], w=[t5], m=True)
                            t7 = kb.op("dve", "tensor_scalar", out=sm[:, 4:5], in0=sm[:, 3:4], scalar1=1.0 / 128,
                                       scalar2=EPS, op0=ALU.mult, op1=ALU.add, w=[t6], m=True)
                            t8 = kb.op("act", "activation", out=sm[:, 5:6], in_=sm[:, 4:5], func=AF.Ln, w=[t7], m=True)
                            t9 = kb.op("act", "activation", out=sm[:, 6:7], in_=sm[:, 5:6], func=AF.Exp, scale=-0.5,
                                       w=[t8], m=True)
                            os_ = on_cnt % 2
                            on_cnt += 1
                            t10 = kb.op("dve", "scalar_tensor_tensor", out=on16[os_][:, :], in0=oo[:, :], scalar=sm[:, 6:7],
                                        in1=gsb2[:, :], op0=ALU.mult, op1=ALU.mult, w=[t9, t_gs, ro.tok(os_)], m=True)
                            t_prev = t10
                            sq = G // 4
                            tile_in_q = (G % 4) * 4 + q4
                            r0 = sq * 320 + tile_in_q * 8
                            ro.dma("sp", os_, XO2[r0:r0 + 8, :].rearrange("r (a e) -> (r a) e", e=128), on16[os_][:, :],
                                   w=[t10])
                kb.barrier([ra, ro])

        def phase3(l, x_src, x_dst, t_ag2, woA, woB, gpb, t_w, rw3):
            with ExitStack() as es:
                def sb(name, shape, dt):
                    return es.enter_context(nc.sbuf_tensor(uniq(name), shape, dt))

                def psb(name, shape, dt):
                    return es.enter_context(nc.psum_tensor(uniq(name), shape, dt))
                NS3 = 3
                mret = [sb(f"mret{i}", [128, 6, 128], BF16) for i in range(NS3)]
                dot = [sb(f"dot{i}", [128, 4, 128], BF16) for i in range(NS3)]
                dgt = [sb(f"dgt{i}", [128, 512], BF16) for i in range(NS3)]
                hTt = [sb(f"hTt{i}", [96, 8, 128], BF16) for i in range(NS3)]
                lgt = [sb(f"lgt{i}", [96, 8, 128], BF16) for i in range(NS3)]
                xt = [sb(f"x3t{i}", [128, D], F32) for i in range(NS3)]
                dmix = [sb(f"dmix{i}", [128, 512], BF16) for i in range(2)]
                dTt = [sb(f"dTt{i}", [128, 4, 128], BF16) for i in range(2)]
                lmix = [sb(f"lmix{i}", [96, 8, 128], BF16) for i in range(2)]
                yn = [sb(f"yn{i}", [128, D], F32) for i in range(2)]
                jk = sb("jk3", [128, 512], BF16)
                s4 = sb("s4", [128, 8], F32)
                py = [psb(f"py{i}", [128, 512], F32) for i in range(4)]
                pt3 = psb("pt3", [128, 1024], BF16)
                rl3 = Ring(kb, f"rl3{l}_", NS3)
                rs3 = Ring(kb, f"rs3{l}_", 2)
                lfree = [[] for _ in range(NS3)]
                py_free = None
                dmix_free = None
                pt3_free = None
                dT_free = None
                lmix_free = None
                ltok = {}

                def loads3(tt):
                    s = tt % NS3
                    c0 = tt * 128
                    rl3.dma("sp", s, mret[s][:, :, :], mixr[:, :, c0:c0 + 128].rearrange("h e t -> e h t"),
                            w=lfree[s] + [t_ag2])
                    rl3.dma("sp", s, dot[s][:, :, :],
                            XOL[:, tt * 8:(tt + 1) * 8, :].rearrange("h r (a e) -> (r a) h e", e=128))
                    for b in range(2):
                        rl3.dma("sp", s, hTt[s][:, :, :].rearrange("c (h b) t -> c h b t", b=2)[:, :, b, :],
                                XOL[:, 128 + b * 96:128 + (b + 1) * 96, c0:c0 + 128].rearrange("h c t -> c h t"))
                    rl3.dma("sp", s, dgt[s][:, :], dg[c0:c0 + 128, :])
                    rl3.dma("sp", s, lgt[s][:, :, :], lgT[:, :, c0:c0 + 128].rearrange("n c t -> c n t"))
                    ltok[tt] = rl3.dma("sp", s, xt[s][:, :], x_src[c0:c0 + 128, :])

                loads3(0)
                loads3(1)
                fr = {}

                def front(tt):
                    s = tt % NS3
                    p = tt % 2
                    t_l = ltok[tt]
                    t_dm = kb.op("dve", "tensor_tensor", out=dmix[p][:, :], in0=dot[s][:, :, :].rearrange("p h e -> p (h e)"),
                                 in1=dgt[s][:, :], op=ALU.mult, w=[t_l, st3["dmix_free"][p]], m=True)
                    for hs in range(4):
                        t_tr = kb.op("pe", "transpose", out=pt3[:, hs * 128:(hs + 1) * 128],
                                     in_=dmix[p][:, hs * 128:(hs + 1) * 128], identity=ident[:, :],
                                     w=[t_dm, st3["pt3_free"]], m=(hs == 3))
                    st3["dmix_free"][p] = t_tr
                    t_dT = kb.op("act", "activation", out=dTt[p][:, :, :].rearrange("p h t -> p (h t)"), in_=pt3[:, 0:512],
                                 func=AF.Copy, w=[t_tr, st3["dT_free"][p]], m=True)
                    st3["pt3_free"] = t_dT
                    t_lm = kb.op("dve", "tensor_tensor", out=lmix[p][:, :, :], in0=hTt[s][:, :, :], in1=lgt[s][:, :, :],
                                 op=ALU.mult, w=[t_l, st3["lmix_free"][p]], m=True)
                    fr[tt] = (t_dm, t_dT, t_lm)

                st3 = dict(dmix_free=[None, None], pt3_free=None, dT_free=[None, None], lmix_free=[None, None])
                front(0)
                for tt in range(NTT):
                    s = tt % NS3
                    p = tt % 2
                    c0 = tt * 128
                    if tt + 2 < NTT:
                        loads3(tt + 2)
                    t_l = ltok[tt]
                    t_dm, t_dT, t_lm = fr[tt]
                    for cg in range(4):
                        cs = slice(cg * 512, (cg + 1) * 512)
                        first = True
                        for h in range(6):
                            kb.op("pe", "matmul", py[cg][:, :], lhsT=mret[s][:, h, :], rhs=woA[:, h, cs], start=first,
                                  stop=False, w=([t_l, t_w, rw3.tok(0), py_free] if first else []))
                            first = False
                        for h in range(4):
                            kb.op("pe", "matmul", py[cg][:, :], lhsT=dTt[p][:, h, :], rhs=woA[:, 6 + h, cs], start=False,
                                  stop=False, w=[t_dT])
                        for nb in range(8):
                            t_mm = kb.op("pe", "matmul", py[cg][:, :], lhsT=lmix[p][:, nb, :], rhs=woB[:, nb, cs],
                                         start=False, stop=(nb == 7), w=[t_lm], m=(nb == 7 and cg == 3))
                    st3["dT_free"][p] = t_mm
                    st3["lmix_free"][p] = t_mm
                    if tt + 1 < NTT:
                        front(tt + 1)
                    for cg in range(4):
                        t_sq = kb.op("act", "activation", out=jk[:, :], in_=py[cg][:, :], func=AF.Square,
                                     accum_out=s4[:, cg:cg + 1], w=[t_mm], m=True)
                    t_r = kb.op("dve", "tensor_reduce", out=s4[:, 4:5], in_=s4[:, 0:4], axis=AX.X, op=ALU.add,
                                w=[t_sq], m=True)
                    t_m = kb.op("dve", "tensor_scalar", out=s4[:, 5:6], in0=s4[:, 4:5], scalar1=1.0 / D, scalar2=EPS,
                                op0=ALU.mult, op1=ALU.add, w=[t_r], m=True)
                    t_ln = kb.op("act", "activation", out=s4[:, 6:7], in_=s4[:, 5:6], func=AF.Ln, w=[t_m], m=True)
                    t_rs = kb.op("act", "activation", out=s4[:, 7:8], in_=s4[:, 6:7], func=AF.Exp, scale=-0.5,
                                 w=[t_ln], m=True)
                    for cg in range(4):
                        cs = slice(cg * 512, (cg + 1) * 512)
                        t_y = kb.op("dve", "scalar_tensor_tensor", out=yn[tt % 2][:, cs], in0=py[cg][:, :], scalar=s4[:, 7:8],
                                    in1=gpb[:, cs], op0=ALU.mult, op1=ALU.mult, w=[t_rs, rs3.tok(tt % 2)], m=True)
                    py_free = t_y
                    t_o = kb.op("dve", "tensor_tensor", out=yn[tt % 2][:, :], in0=yn[tt % 2][:, :], in1=xt[s][:, :], op=ALU.add,
                                w=[t_y], m=True)
                    lfree[s] = [t_o, t_mm, t_lm, t_dm]
                    rs3.dma("pool", tt % 2, x_dst[c0:c0 + 128, :], yn[tt % 2][:, :], w=[t_o])
                kb.barrier([rw3, rl3, rs3])

        for l in range(nlayers):
            x_src = x_in if l == 0 else x1
            x_dst = x1 if l == 0 and nlayers > 1 else out_t
            ag_box = []
            phase1(l, x_src, ag_box)
            if stop_after == ("p1", l):
                break
            t_ag1 = copy_local(XDG[l], XDL, 576, ag_box[0])
            phase2(l, t_ag1)
            if stop_after == ("p2", l):
                break
            with ExitStack() as es3:
                woA = es3.enter_context(nc.sbuf_tensor(uniq("woA"), [128, 10, D], BF16))
                woB = es3.enter_context(nc.sbuf_tensor(uniq("woB"), [96, 8, D], BF16))
                gpb = es3.enter_context(nc.sbuf_tensor(uniq("gpb"), [128, D], F32))
                rw3 = Ring(kb, f"rw3{l}_", 1)
                rw3.dma("sp", 0, woA[:, :, :], WO[l][0:1280, :].rearrange("(k p) c -> p k c", p=128), w=[ensure_w(l, "o")])
                rw3.dma("sp", 0, woB[:, :, :], WO[l][1280:2048, :].rearrange("(k p) c -> p k c", p=96))
                t_w = rw3.dma("sp", 0, gpb[:, :], post_g[l, :].partition_broadcast(128))
                ag2_box = []
                phaseR(l, lambda: ag2_box.append(allgather(XO, XOG[l], [], groups=ALL8)))
                if stop_after == ("r", l):
                    break
                t_ag2 = ag2_box[0]
                t_ag2 = copy_local(XOG[l], XOL, 320, t_ag2)
                if stop_after == ("c2", l):
                    kb.wait("pool", t_ag2)
                    break
                phase3(l, x_src, x_dst, t_ag2, woA, woB, gpb, t_w, rw3)
        if debug:
            fin.dma("sp", 0, dbgs["XDd"][:, :], XD[:, :])
            fin.dma("sp", 0, dbgs["XOd"][:, :], XO[:, :])
        kb.barrier([fin])
    return nc


def _consts():
    theta = np.float32(10000.0)
    inv_r = (np.float32(1.0) / (theta ** np.linspace(0.0, 1.0, 64, dtype=np.float32))).astype(np.float32)
    inv_d = (np.float32(1.0) / (theta ** (np.arange(0, 64, 2, dtype=np.float32) / np.float32(64)))).astype(np.float32)
    invf = np.concatenate([inv_r, inv_d]).astype(np.float32)
    hh = np.arange(6, dtype=np.float64)
    log_g = np.log1p(-np.power(2.0, -5.0 - hh))
    p1 = np.arange(1, 129, dtype=np.float64)[:, None]
    dq = np.exp(p1 * log_g[None, :])
    dk = np.exp(-p1 * log_g[None, :]) * (128.0 ** -0.5)
    dec = np.concatenate([dq, dk], axis=1).astype(np.float32)
    gc = np.exp(128.0 * log_g).astype(np.float32)
    gR = np.exp(2048.0 * log_g)
    cfs = []
    for r in range(4):
        cf = np.zeros((3, 6), np.float64)
        for i in range(3):
            if i < r:
                cf[i] = gR ** (r - 1 - i)
        cfs.append(cf.reshape(-1).astype(np.float32))
    ident = np.eye(128, dtype=np.float32).astype(ml_dtypes.bfloat16)
    jj = np.arange(128)[:, None]
    ii = np.arange(128)[None, :]
    mask = (jj <= ii).astype(np.float32).astype(ml_dtypes.bfloat16)
    return invf, dec, gc, cfs, ident, mask


def make_in_maps(inputs):
    invf, dec, gc, cfs, ident, mask = _consts()
    x = np.asarray(inputs["x"], dtype=np.float32)
    pos = np.asarray(inputs["positions"]).astype(np.int32)
    w_in = np.asarray(inputs["w_in"], dtype=np.float32)
    w_out = np.asarray(inputs["w_out"], dtype=np.float32)
    RW = D // NCORES
    pre_g = np.ascontiguousarray(np.asarray(inputs["pre_norm_g"], dtype=np.float32))
    post_g = np.ascontiguousarray(np.asarray(inputs["post_norm_g"], dtype=np.float32))
    dlam = np.stack([np.asarray(inputs[k], dtype=np.float32) for k in
                     ("diff_lambda_q1", "diff_lambda_k1", "diff_lambda_q2", "diff_lambda_k2")], axis=1)
    subg = np.ascontiguousarray(np.asarray(inputs["diff_subln_g"], dtype=np.float32))
    conv_w = np.asarray(inputs["lru_conv_w"], dtype=np.float32)
    conv_b = np.asarray(inputs["lru_conv_b"], dtype=np.float32)
    wa = np.asarray(inputs["lru_wa"], dtype=np.float32)
    wx = np.asarray(inputs["lru_wx"], dtype=np.float32)
    ba = np.asarray(inputs["lru_ba"], dtype=np.float32)
    bx = np.asarray(inputs["lru_bx"], dtype=np.float32)
    lam = np.asarray(inputs["lru_lambda"], dtype=np.float32)
    in_maps = []
    for c in range(NCORES):
        b, r = c // 4, c % 4
        xs = np.ascontiguousarray(x[b, r * T:(r + 1) * T, :])
        ps = np.ascontiguousarray(pos[b, r * T:(r + 1) * T].reshape(NTT, 128).T)
        lw = np.zeros((L, 2, 2, 96, 96), np.float32)
        lp = np.zeros((L, 2, 96, 8), np.float32)
        for bb in range(2):
            nb = 2 * r + bb
            sl = slice(nb * 96, (nb + 1) * 96)
            lw[:, bb, 0] = wa[:, nb]
            lw[:, bb, 1] = wx[:, nb]
            lp[:, bb, :, 0:4] = np.transpose(conv_w[:, :, sl], (0, 2, 1))
            lp[:, bb, :, 4] = conv_b[:, sl]
            lp[:, bb, :, 5] = ba[:, sl]
            lp[:, bb, :, 6] = bx[:, sl]
            lp[:, bb, :, 7] = lam[:, sl]
        in_maps.append({
            "x_c": xs, "pos_c": ps,
            "w_in_s": np.ascontiguousarray(w_in[:, c * RW:(c + 1) * RW, :]),
            "w_out_s": np.ascontiguousarray(w_out[:, c * RW:(c + 1) * RW, :]), "pre_g": pre_g, "post_g": post_g,
            "dlam": np.ascontiguousarray(dlam), "subg": subg, "lru_w": lw, "lru_p": lp,
            "c_ident": ident, "c_mask": mask, "c_invf": invf, "c_dec": dec, "c_gc": gc, "c_cf": cfs[r],
        })
    return in_maps


_NC_CACHE = {}


def kernel(**inputs):
    if "nc" not in _NC_CACHE:
        _NC_CACHE["nc"] = build()
    nc = _NC_CACHE["nc"]
    in_maps = make_in_maps(inputs)
    res = run_bass_kernel_spmd(nc, in_maps, core_ids=list(range(NCORES)))
    out = np.empty((2, S, D), np.float32)
    for c in range(NCORES):
        b, r = c // 4, c % 4
        out[b, r * T:(r + 1) * T, :] = res.results[c]["out"]
    return out
```
